# Optimizing a Trainium2 kernel written in Bass

```python
import math
import jax, jax.numpy as jnp
from jax import lax
import numpy as np

D_MODEL = 2048
BATCH = 4
SEQ = 8192
DEPTH = 2

MEM_LEN = 256
HEAD_DIM = 64
A_WIDTH = 3 * D_MODEL // 8
A_HEADS = A_WIDTH // HEAD_DIM
A_PATTERNS = ((128, 1), (512, 4), (2048, 16))
B_WIDTH = D_MODEL // 4
B_HEADS = B_WIDTH // HEAD_DIM
BLOCK_Q = 128
C_HEADS = 4
C_VW = 3 * D_MODEL // 8
C_DV = C_VW // C_HEADS
C_DK = C_DV // 2
C_KW = C_HEADS * C_DK
C_GATE_RANK = 16
C_GATE_TAU = 16.0
C_CHUNK = 64
MIX_WIDTH = A_WIDTH + B_WIDTH + C_VW
PROJ_SPLITS = (A_WIDTH, A_WIDTH, A_WIDTH, B_WIDTH, B_WIDTH, B_WIDTH, B_HEADS, C_KW, C_KW, C_VW, C_VW, C_GATE_RANK)
PROJ_WIDTH = sum(PROJ_SPLITS)
CROSS_HEADS = 4
CROSS_DH = 128
CROSS_WIDTH = CROSS_HEADS * CROSS_DH
D_FF = ((8 * D_MODEL // 3 + 127) // 128) * 128
CONV_W = 3
REL_BUCKETS = 32
REL_MAX_DIST = 2048
EPS = 1e-6

kernel_name = 'hybrid_dilated_fox_gla_block'


def rms_norm(x, g):
    xf = x.astype(jnp.float32)
    y = xf * lax.rsqrt(jnp.mean(xf * xf, axis=-1, keepdims=True) + EPS)
    return (y * g.astype(jnp.float32)).astype(x.dtype)


def t5_bucket(dist):
    max_exact = REL_BUCKETS // 2
    d = np.maximum(dist, 1).astype(np.float32)
    large = max_exact + (np.log(d / max_exact) / math.log(REL_MAX_DIST / max_exact)
                         * (REL_BUCKETS - max_exact)).astype(np.int32)
    return np.where(dist < max_exact, dist, np.minimum(large, REL_BUCKETS - 1)).astype(np.int32)


def dilated_pattern(q, k, v, rel_table, window, dilation):
    B, S, H, Dh = q.shape
    span = window // dilation
    L = S // dilation
    nb = -(-L // span)
    Lp = nb * span

    def classes(t):
        t = t.reshape(B, L, dilation, H, Dh).transpose(0, 3, 2, 1, 4)
        return jnp.pad(t, ((0, 0), (0, 0), (0, 0), (0, Lp - L), (0, 0)))

    def band(t):
        t = jnp.pad(classes(t), ((0, 0), (0, 0), (0, 0), (span, 0), (0, 0)))
        t = t.reshape(B, H, dilation, nb + 1, span, Dh)
        return jnp.concatenate([t[:, :, :, :-1], t[:, :, :, 1:]], axis=4)

    qb = classes(q).reshape(B, H, dilation, nb, span, Dh)
    kb, vb = band(k), band(v)
    i = np.arange(span)[:, None]
    j = np.arange(2 * span)[None, :]
    step = i + span - j
    valid = (step >= 0) & (step <= span) & (np.arange(nb)[:, None, None] * span + j - span >= 0)
    bias = rel_table[t5_bucket(np.clip(step, 0, span) * dilation)]
    bias = jnp.transpose(bias, (2, 0, 1)).astype(jnp.float32)
    logits = jnp.einsum('bhrnid,bhrnjd->bhrnij', qb, kb,
                        preferred_element_type=jnp.float32) * HEAD_DIM ** -0.5
    logits = jnp.where(valid, logits + bias[None, :, None, None], -jnp.inf)
    m = jnp.max(logits, axis=-1, keepdims=True)
    p = jnp.exp(logits - m)
    s = jnp.sum(p, axis=-1, keepdims=True)
    o = jnp.einsum('bhrnij,bhrnjd->bhrnid', p.astype(v.dtype), vb,
                   preferred_element_type=jnp.float32) / s
    lse = (m + jnp.log(s))[..., 0]
    o = o.reshape(B, H, dilation, Lp, Dh)[:, :, :, :L].transpose(0, 3, 2, 1, 4).reshape(B, S, H, Dh)
    lse = lse.reshape(B, H, dilation, Lp)[:, :, :, :L].transpose(0, 3, 2, 1).reshape(B, S, H)
    return o, lse


def dilated_mixture_attention(q, k, v, rel_table):
    outs, lses = [], []
    for window, dilation in A_PATTERNS:
        o, lse = dilated_pattern(q, k, v, rel_table, window, dilation)
        outs.append(o)
        lses.append(lse)
    w = jax.nn.softmax(jnp.stack(lses, axis=0), axis=0)
    return jnp.sum(w[..., None] * jnp.stack(outs, axis=0), axis=0)


def forgetting_attention(q, k, v, f_logit):
    B, S, H, Dh = q.shape
    c = jnp.cumsum(jax.nn.log_sigmoid(f_logit.astype(jnp.float32)), axis=1)
    cT = c.transpose(0, 2, 1)
    nq = S // BLOCK_Q
    qb = q.reshape(B, nq, BLOCK_Q, H, Dh).transpose(1, 0, 2, 3, 4)
    cb = cT.reshape(B, H, nq, BLOCK_Q).transpose(2, 0, 1, 3)
    kpos = jnp.arange(S)

    def block(args):
        n, qn, cn = args
        logits = jnp.einsum('bihd,bjhd->bhij', qn, k,
                            preferred_element_type=jnp.float32) * HEAD_DIM ** -0.5
        logits = logits + (cn[..., :, None] - cT[..., None, :])
        qpos = n * BLOCK_Q + jnp.arange(BLOCK_Q)
        logits = jnp.where(kpos[None, :] <= qpos[:, None], logits, -jnp.inf)
        p = jax.nn.softmax(logits, axis=-1)
        return jnp.einsum('bhij,bjhd->bihd', p.astype(v.dtype), v)

    ob = lax.map(block, (jnp.arange(nq), qb, cb))
    return ob.transpose(1, 0, 2, 3, 4).reshape(B, S, H, Dh)


def gla_chunked(q, k, v, log_alpha):
    B, S, H, DK = q.shape
    nc = S // C_CHUNK

    def chunks(t):
        return t.astype(jnp.float32).reshape(B, nc, C_CHUNK, H, t.shape[-1]).transpose(1, 0, 3, 2, 4)

    qc, kc, vc, gc = chunks(q * DK ** -0.5), chunks(k), chunks(v), chunks(log_alpha)
    causal = np.tril(np.ones((C_CHUNK, C_CHUNK), dtype=bool))[:, :, None]

    def step(state, inp):
        qn, kn, vn, gn = inp
        b = jnp.cumsum(gn, axis=2)
        o_inter = jnp.einsum('bhik,bhkv->bhiv', qn * jnp.exp(b), state)
        diff = b[:, :, :, None, :] - b[:, :, None, :, :]
        decay = jnp.exp(jnp.where(causal, diff, -jnp.inf))
        a = jnp.einsum('bhik,bhjk,bhijk->bhij', qn, kn, decay)
        o_intra = jnp.einsum('bhij,bhjv->bhiv', a, vn)
        b_last = b[:, :, -1:, :]
        state = state * jnp.exp(b_last[:, :, 0, :])[..., None] + \
            jnp.einsum('bhjk,bhjv->bhkv', kn * jnp.exp(b_last - b), vn)
        return state, o_inter + o_intra

    state0 = jnp.zeros((B, H, DK, v.shape[-1]), jnp.float32)
    _, o = lax.scan(step, state0, (qc, kc, vc, gc))
    return o.transpose(1, 0, 3, 2, 4).reshape(B, S, H, v.shape[-1])


def hybrid_mixer(h, w_in, f_bias, rel_table, c_gate_w2, c_gate_b, c_norm, w_out):
    B, S, _ = h.shape
    proj = h @ w_in
    qa, ka, va, qb, kb, vb, fl, qc, kc, vc, rc, gl = jnp.split(
        proj, np.cumsum(PROJ_SPLITS)[:-1].tolist(), axis=-1)

    def heads(t, n):
        return t.reshape(B, S, n, -1)

    o_a = dilated_mixture_attention(heads(qa, A_HEADS), heads(ka, A_HEADS), heads(va, A_HEADS), rel_table)
    o_b = forgetting_attention(heads(qb, B_HEADS), heads(kb, B_HEADS), heads(vb, B_HEADS), fl + f_bias)
    log_alpha = jax.nn.log_sigmoid((gl @ c_gate_w2 + c_gate_b).astype(jnp.float32)) / C_GATE_TAU
    o_c = gla_chunked(heads(qc, C_HEADS), heads(kc, C_HEADS), heads(vc, C_HEADS), heads(log_alpha, C_HEADS))
    o_c = rms_norm(o_c.astype(h.dtype), c_norm).reshape(B, S, C_VW) * jax.nn.silu(rc)
    mixed = jnp.concatenate([o_a.reshape(B, S, A_WIDTH).astype(h.dtype),
                             o_b.reshape(B, S, B_WIDTH).astype(h.dtype),
                             o_c.astype(h.dtype)], axis=-1)
    return mixed @ w_out


def memory_cross_attention(h, mem_n, w_q, w_kv, w_o):
    B, S, _ = h.shape
    M = mem_n.shape[1]
    q = (h @ w_q).reshape(B, S, CROSS_HEADS, CROSS_DH)
    k, v = jnp.split(mem_n @ w_kv, 2, axis=-1)
    k = k.reshape(B, M, CROSS_HEADS, CROSS_DH)
    v = v.reshape(B, M, CROSS_HEADS, CROSS_DH)
    logits = jnp.einsum('bshd,bmhd->bhsm', q, k, preferred_element_type=jnp.float32) * CROSS_DH ** -0.5
    p = jax.nn.softmax(logits, axis=-1)
    o = jnp.einsum('bhsm,bmhd->bshd', p.astype(v.dtype), v).reshape(B, S, CROSS_WIDTH)
    return o @ w_o


def conv_ffn(h, w_up, conv_w, conv_b, w_down):
    S = h.shape[1]
    val, gate = jnp.split(h @ w_up, 2, axis=-1)
    gp = jnp.pad(gate, ((0, 0), (CONV_W - 1, 0), (0, 0)))
    g = conv_b
    for i in range(CONV_W):
        g = g + conv_w[i] * gp[:, i:i + S]
    return (jax.nn.silu(g) * val) @ w_down


def setup_inputs(seed: int = 0) -> dict:
    key = jax.random.key(seed)
    ks = jax.random.split(key, 21)

    def nrm(k, shape, scale):
        return jax.random.normal(k, shape, jnp.float32) * scale

    def gain(k, shape):
        return 1.0 + 0.05 * jax.random.normal(k, shape, jnp.float32)

    return {
        'x': nrm(ks[0], (BATCH, SEQ, D_MODEL), 1.0),
        'mem': nrm(ks[1], (BATCH, MEM_LEN, D_MODEL), 1.0),
        'rel_table': nrm(ks[2], (REL_BUCKETS, A_HEADS), 0.5),
        'mem_norm': gain(ks[3], (D_MODEL,)),
        'norm_final': gain(ks[4], (D_MODEL,)),
        'norm_mix': gain(ks[5], (DEPTH, D_MODEL)),
        'w_in': nrm(ks[6], (DEPTH, D_MODEL, PROJ_WIDTH), D_MODEL ** -0.5),
        'f_bias': 3.0 + nrm(ks[7], (DEPTH, B_HEADS), 0.1),
        'c_gate_w2': nrm(ks[8], (DEPTH, C_GATE_RANK, C_KW), C_GATE_RANK ** -0.5),
        'c_gate_b': nrm(ks[9], (DEPTH, C_KW), 0.1),
        'c_norm': gain(ks[10], (DEPTH, C_DV)),
        'w_out': nrm(ks[11], (DEPTH, MIX_WIDTH, D_MODEL), MIX_WIDTH ** -0.5),
        'norm_cross': gain(ks[12], (DEPTH, D_MODEL)),
        'w_cq': nrm(ks[13], (DEPTH, D_MODEL, CROSS_WIDTH), D_MODEL ** -0.5),
        'w_ckv': nrm(ks[14], (DEPTH, D_MODEL, 2 * CROSS_WIDTH), D_MODEL ** -0.5),
        'w_co': nrm(ks[15], (DEPTH, CROSS_WIDTH, D_MODEL), CROSS_WIDTH ** -0.5),
        'norm_ffn': gain(ks[16], (DEPTH, D_MODEL)),
        'w_up': nrm(ks[17], (DEPTH, D_MODEL, 2 * D_FF), D_MODEL ** -0.5),
        'conv_w': nrm(ks[18], (DEPTH, CONV_W, D_FF), CONV_W ** -0.5),
        'conv_b': nrm(ks[19], (DEPTH, D_FF), 0.02),
        'w_down': nrm(ks[20], (DEPTH, D_FF, D_MODEL), D_FF ** -0.5),
    }


def reference(x, mem, rel_table, mem_norm, norm_final, norm_mix, w_in, f_bias, c_gate_w2,
              c_gate_b, c_norm, w_out, norm_cross, w_cq, w_ckv, w_co, norm_ffn, w_up,
              conv_w, conv_b, w_down):
    mem_n = rms_norm(mem, mem_norm)
    for l in range(DEPTH):
        x = x + hybrid_mixer(rms_norm(x, norm_mix[l]), w_in[l], f_bias[l], rel_table,
                             c_gate_w2[l], c_gate_b[l], c_norm[l], w_out[l])
        x = x + memory_cross_attention(rms_norm(x, norm_cross[l]), mem_n, w_cq[l], w_ckv[l], w_co[l])
        x = x + conv_ffn(rms_norm(x, norm_ffn[l]), w_up[l], conv_w[l], conv_b[l], w_down[l])
    return rms_norm(x, norm_final)
```

```python
import contextlib
import math
import numpy as np
import concourse.bass as bass
import concourse.mybir as mybir
from concourse.bass_utils import run_bass_kernel_spmd

F32 = mybir.dt.float32
BF16 = mybir.dt.bfloat16
ALU = mybir.AluOpType
AF = mybir.ActivationFunctionType
AX = mybir.AxisListType

D = 2048
KC = D // 128
HD = 64
A_W, A_H = 768, 12
B_W, B_H = 512, 8
C_H, C_DV, C_DK, C_KW, C_VW = 4, 192, 96, 384, 768
C_RANK = 16
PROJ_W = 6168
O_QA, O_KA, O_VA, O_QB, O_KB, O_VB, O_FL, O_QC, O_KC, O_VC, O_RC, O_GL = (
    0, 768, 1536, 2304, 2816, 3328, 3840, 3848, 4232, 4616, 5384, 6152)
MEM = 256
X_H, X_DH, X_W = 4, 128, 512
DFF = 5504
FC = DFF // 128
EPS = 1e-6
NEG = -30000.0
SELF_SYNC = True


class Sem:
    __slots__ = ("name", "h", "count", "is_dma")

    def __init__(self, name, h, is_dma):
        self.name, self.h, self.count, self.is_dma = name, h, 0, is_dma


class Buf:
    __slots__ = ("name", "w", "r")

    def __init__(self, name=""):
        self.name, self.w, self.r = name, {}, {}


class Instr:
    __slots__ = ("eng", "fn", "deps", "signal", "sem", "val", "idx")

    def __init__(self, eng, fn, sem):
        self.eng, self.fn, self.sem = eng, fn, sem
        self.deps, self.signal, self.val = {}, False, None


class KB:
    ENGS = ("pe", "act", "dve", "pool", "sp")
    BLK = {"pe": "tensor", "act": "scalar", "dve": "vector", "pool": "gpsimd", "sp": "sync"}

    def __init__(self, nc, stack, arena_words=48 * 1024):
        self.nc, self.stack = nc, stack
        self.prog = {e: [] for e in self.ENGS}
        self.esem = {e: self.new_sem("e_" + e, False) for e in self.ENGS}
        self.n = 0
        self.last = {}
        self.pend = {e: {} for e in self.ENGS}
        self.arena = stack.enter_context(nc.sbuf_tensor("arena", [128, arena_words], F32))
        self.arena_words = arena_words
        self.top = 0
        self.psum = stack.enter_context(nc.psum_tensor("psum", [128, 8, 512], F32))
        self.pbank = [Buf("bank%d" % i) for i in range(8)]
        self.dq = 0

    def new_sem(self, name, is_dma=True):
        if not hasattr(self, "sems"):
            self.sems = []
            self.free_sems = []
        if is_dma and self.free_sems:
            return self.free_sems.pop()
        sm = Sem(name, self.stack.enter_context(self.nc.semaphore(name)), is_dma)
        self.sems.append(sm)
        return sm

    def alloc(self, shape, dt):
        p = shape[0]
        free = int(np.prod(shape[1:]))
        words = free if dt == F32 else (free + 1) // 2
        words = (words + 7) // 8 * 8
        assert self.top + words <= self.arena_words, ("SBUF arena overflow", self.top, words)
        v = self.arena[0:p, self.top:self.top + words]
        self.top += words
        if dt != F32:
            v = v.bitcast(dt)[:, 0:free]
        else:
            v = v[:, 0:free]
        if len(shape) == 3:
            v = v.rearrange("p (a b) -> p a b", a=shape[1])
        elif len(shape) == 4:
            v = v.rearrange("p (a b c) -> p a b c", a=shape[1], b=shape[2])
        return v

    def phase(self):
        self.top = 0
        self.free_sems = [sm for sm in self.sems if sm.is_dma]
        snap = dict(self.last)
        for e in self.ENGS:
            for s, i in snap.items():
                o = self.pend[e].get(s)
                if o is None or o.idx < i.idx:
                    self.pend[e][s] = i
        for i in snap.values():
            i.signal = True

    def _rec(self, eng, fn, sem, reads, writes, skip_self, touch=()):
        ins = Instr(eng, fn, sem)
        ins.idx = self.n
        self.n += 1
        deps = ins.deps

        def add(ev):
            for s, i in ev.items():
                if skip_self and s is sem:
                    continue
                o = deps.get(s)
                if o is None or o.idx < i.idx:
                    deps[s] = i

        if self.pend[eng]:
            add(self.pend[eng])
            self.pend[eng] = {}
        for b in reads:
            add(b.w)
        for b in writes:
            add(b.w)
            add(b.r)
        for i in deps.values():
            i.signal = True
        for b in reads:
            b.r[sem] = ins
        for b in writes:
            b.w[sem] = ins
        for b in touch:
            b.w[sem] = ins
        self.prog[eng].append(ins)
        self.last[sem] = ins
        return ins

    def op(self, eng, fn, reads=(), writes=()):
        skip = (eng == "pe") or (not SELF_SYNC)
        return self._rec(eng, fn, self.esem[eng], reads, writes, skip)

    def mm(self, out, lhsT, rhs, start, stop, reads, writes, **kw):
        return self.op("pe", lambda e: e.matmul(out, lhsT=lhsT, rhs=rhs, start=start, stop=stop, **kw), reads, writes)

    def tr(self, out, in_, ident, reads, writes):
        return self.op("pe", lambda e: e.transpose(out, in_, ident), reads, writes)

    def copy(self, eng, out, in_, reads, writes):
        if eng == "act":
            return self.op("act", lambda e: e.copy(out=out, in_=in_), reads, writes)
        return self.op(eng, lambda e: e.tensor_copy(out=out, in_=in_), reads, writes)

    def act(self, out, in_, func, reads, writes, **kw):
        return self.op("act", lambda e: e.activation(out=out, in_=in_, func=func, **kw), reads, writes)

    def tt(self, eng, out, in0, in1, op, reads, writes):
        return self.op(eng, lambda e: e.tensor_tensor(out=out, in0=in0, in1=in1, op=op), reads, writes)

    def ts(self, eng, out, in0, s1, s2, op0, op1, reads, writes):
        if op1 is None:
            return self.op(eng, lambda e: e.tensor_scalar(out=out, in0=in0, scalar1=s1, scalar2=None, op0=op0), reads, writes)
        return self.op(eng, lambda e: e.tensor_scalar(out=out, in0=in0, scalar1=s1, scalar2=s2, op0=op0, op1=op1), reads, writes)

    def stt(self, eng, out, in0, scalar, in1, op0, op1, reads, writes):
        return self.op(eng, lambda e: e.scalar_tensor_tensor(out=out, in0=in0, scalar=scalar, in1=in1, op0=op0, op1=op1), reads, writes)

    def recip(self, out, in_, reads, writes):
        return self.op("dve", lambda e: e.reciprocal(out=out, in_=in_), reads, writes)

    def memset(self, eng, out, val, writes):
        return self.op(eng, lambda e: e.memset(out, val), (), writes)

    def dma(self, out, in_, sem, reads=(), writes=(), q=None, touch=(), **kw):
        if q is None:
            q = ("sp", "pool")[self.dq & 1]
            self.dq += 1

        def fn(e, out=out, in_=in_, kw=kw):
            return e.dma_start(out=out, in_=in_, **kw)
        ins = self._rec(q, fn, sem, reads, writes, True, touch)
        ins.signal = True
        return ins

    def finalize(self):
        final_sems = [sm for sm in self.sems if sm.is_dma]
        allins = sorted((i for e in self.ENGS for i in self.prog[e]), key=lambda i: i.idx)
        for i in allins:
            if i.signal:
                i.sem.count += 16 if i.sem.is_dma else 1
                i.val = i.sem.count
        with self.nc.Block() as block:
            for e in self.ENGS:
                prog = self.prog[e]
                fin = final_sems if e == "sp" else ()

                def body(eng, prog=prog, fin=fin):
                    known = {}
                    for ins in prog:
                        for s, d in ins.deps.items():
                            if known.get(s, 0) < d.val:
                                eng.wait_ge(s.h, d.val)
                                known[s] = d.val
                        r = ins.fn(eng)
                        if ins.signal:
                            r.then_inc(ins.sem.h, 16 if ins.sem.is_dma else 1)
                    for s in fin:
                        if s.count > 0:
                            eng.wait_ge(s.h, s.count)

                getattr(block, self.BLK[e])(body)
        return self.nc


class Ring:
    def __init__(self, kb, name, n, shape, dt, dma=True):
        self.t = [kb.alloc(shape, dt) for _ in range(n)]
        self.b = [Buf("%s%d" % (name, i)) for i in range(n)]
        self.s = [kb.new_sem("%s%d" % (name, i)) for i in range(n)] if dma else [None] * n
        self.i = -1
        self.n = n

    def next(self):
        self.i = (self.i + 1) % self.n
        return self.t[self.i], self.b[self.i], self.s[self.i]


class PsRing:
    def __init__(self, kb, banks):
        self.kb, self.banks, self.i = kb, banks, -1

    def next(self):
        self.i = (self.i + 1) % len(self.banks)
        k = self.banks[self.i]
        return self.kb.psum[:, k, :], self.kb.pbank[k]


def load_cast(kb, stage, dst, dstbuf, src_ap, shape_sl, cast_eng):
    st, sb, ss = stage.next()
    sv = st[shape_sl]
    kb.dma(sv, src_ap, ss, writes=[sb])
    kb.copy(cast_eng, dst, sv, [sb], [dstbuf])


class NormPass:
    def __init__(self, kb, consts, TBN=256, banks=(0, 1)):
        self.kb, self.consts, self.TBN = kb, consts, TBN
        self.xs = Ring(kb, "xs", 2, [128, KC, TBN], F32)
        self.sq = Ring(kb, "sq", 2, [128, TBN], BF16, dma=False)
        self.rs = Ring(kb, "rs", 2, [128, TBN], F32, dma=False)
        self.pr = PsRing(kb, list(banks))

    def run(self, xT, gvec, gb, T0, NT, hT, hbuf):
        kb, consts, TBN = self.kb, self.consts, self.TBN
        ones_bf, cbuf = consts["ones_bf"], consts["buf"]
        for b in range(NT // TBN):
            t0 = T0 + b * TBN
            xt, xb, xsem = self.xs.next()
            kb.dma(xt, xT[:, t0:t0 + TBN].rearrange("(k p) t -> p k t", p=128), xsem, writes=[xb])
            ps, pb = self.pr.next()
            psv = ps[:, 0:TBN]
            for k in range(KC):
                sqt, sqb, _ = self.sq.next()
                kb.tt("pool" if k % 2 == 0 else "dve", sqt, xt[:, k, :], xt[:, k, :], ALU.mult, [xb], [sqb])
                kb.mm(psv, ones_bf, sqt, k == 0, k == KC - 1, [sqb, cbuf], [pb])
            rt, rb, _ = self.rs.next()
            kb.act(rt, psv, AF.Sqrt, [pb, cbuf], [rb], scale=1.0 / D, bias=consts["eps"])
            kb.recip(rt, rt, [rb], [rb])
            for k in range(KC):
                kb.stt("dve", hT[:, k, b * TBN:(b + 1) * TBN], xt[:, k, :],
                       gvec[:, k:k + 1], rt, ALU.mult, ALU.mult, [xb, rb, gb], [hbuf])


def make_consts(kb):
    c = {}
    cb = Buf("consts")
    c["ones_bf"] = kb.alloc([128, 128], BF16)
    kb.memset("dve", c["ones_bf"], 1.0, [cb])
    c["eps"] = kb.alloc([128, 1], F32)
    kb.memset("dve", c["eps"], EPS, [cb])
    c["buf"] = cb
    return c


FM_GROUPS = (
    ("qAT", O_QA, 128, 6), ("kAT", O_KA, 128, 6), ("qBT", O_QB, 128, 4), ("kBT", O_KB, 128, 4),
    ("qCT", O_QC, 96, 4), ("kCT", O_KC, 96, 4), ("glT", O_GL, 16, 1))
TM_GROUPS = (
    ("vA", O_VA, 768, 384), ("vB", O_VB, 512, 512), ("fl", O_FL, 8, 8), ("kC", O_KC, 384, 384),
    ("vC", O_VC, 768, 384), ("rC", O_RC, 768, 384))


def phase_p1(kb, T, io, NS=2048):
    NS = min(NS, T)
    consts = make_consts(kb)
    gvec = kb.alloc([128, KC], F32)
    gb = Buf("g")
    gsem = kb.new_sem("gsem")
    kb.dma(gvec, io["g1"].rearrange("(k p) -> p k", p=128), gsem, writes=[gb], allow_slow_non_contiguous=True)
    hT = kb.alloc([128, KC, NS], BF16)
    hbuf = Buf("hT")
    normp = NormPass(kb, consts)
    wbf = Ring(kb, "wbf", 3, [128, KC, 512], BF16)
    ofm = Ring(kb, "ofm", 2, [128, NS], BF16)
    ofm32 = Ring(kb, "ofm32", 1, [16, NS], F32)
    otm = Ring(kb, "otm", 3, [128, 512], BF16)
    otm32 = Ring(kb, "otm32", 2, [128, 8], F32)
    pr = PsRing(kb, [2, 3, 4, 5, 6, 7])
    obuf = Buf("p1out")
    ci = 0
    for sb in range(T // NS):
        T0 = sb * NS
        normp.run(io["xT"], gvec, gb, T0, NS, hT, hbuf)
        for name, c0, rows, nblk in FM_GROUPS:
            dst = io[name]
            for j in range(nblk):
                wt, wb, wsem_ = wbf.next()
                wv = wt[:, :, 0:rows]
                kb.dma(wv, io["w_in"][:, c0 + j * rows:c0 + (j + 1) * rows].rearrange("(k p) m -> p k m", p=128), wsem_, writes=[wb])
                ot, ob, os_ = (ofm32 if name == "glT" else ofm).next()
                for tb in range(NS // 512):
                    ps, pb = pr.next()
                    for k in range(KC):
                        kb.mm(ps[0:rows, :], wv[:, k, :], hT[:, k, tb * 512:(tb + 1) * 512], k == 0, k == KC - 1, [wb, hbuf], [pb])
                    kb.copy(("dve", "act")[tb & 1], ot[0:rows, tb * 512:(tb + 1) * 512], ps[0:rows, :], [pb], [ob])
                kb.dma(dst[j * rows:(j + 1) * rows, T0:T0 + NS], ot[0:rows, :], os_, reads=[ob], touch=[obuf])
        for name, c0, tot, cw in TM_GROUPS:
            dst = io[name]
            for j in range(tot // cw):
                wt, wb, wsem_ = wbf.next()
                wv = wt[:, :, 0:cw]
                kb.dma(wv, io["w_in"][:, c0 + j * cw:c0 + (j + 1) * cw].rearrange("(k p) m -> p k m", p=128), wsem_, writes=[wb])
                for tt in range(NS // 128):
                    ps, pb = pr.next()
                    for k in range(KC):
                        kb.mm(ps[:, 0:cw], hT[:, k, tt * 128:(tt + 1) * 128], wv[:, k, :], k == 0, k == KC - 1, [wb, hbuf], [pb])
                    ot, ob, os_ = (otm32 if name == "fl" else otm).next()
                    kb.copy(("dve", "act")[tt & 1], ot[:, 0:cw], ps[:, 0:cw], [pb], [ob])
                    kb.dma(dst[T0 + tt * 128:T0 + (tt + 1) * 128, j * cw:(j + 1) * cw], ot[:, 0:cw], os_, reads=[ob], touch=[obuf])
    return obuf


class NormSB:
    def __init__(self, kb, consts, n, banks=(0, 1)):
        self.kb, self.consts, self.n = kb, consts, n
        self.sq = Ring(kb, "nsq", 2, [128, n], BF16, dma=False)
        self.rs = Ring(kb, "nrs", 1, [128, n], F32, dma=False)
        self.pr = PsRing(kb, list(banks))

    def rstd(self, xt, xb, c0, n):
        kb, consts = self.kb, self.consts
        ps, pb = self.pr.next()
        psv = ps[:, 0:n]
        for k in range(KC):
            sqt, sqb, _ = self.sq.next()
            kb.tt("pool" if k % 2 == 0 else "dve", sqt[:, 0:n], xt[:, k, c0:c0 + n], xt[:, k, c0:c0 + n], ALU.mult, [xb], [sqb])
            kb.mm(psv, consts["ones_bf"], sqt[:, 0:n], k == 0, k == KC - 1, [sqb, consts["buf"]], [pb])
        rt, rb, _ = self.rs.next()
        rv = rt[:, 0:n]
        kb.act(rv, psv, AF.Sqrt, [pb, consts["buf"]], [rb], scale=1.0 / D, bias=consts["eps"])
        kb.recip(rv, rv, [rb], [rb])
        return rv, rb

    def apply(self, xt, xb, c0, n, gvec, gb, out, obuf, oc0):
        rv, rb = self.rstd(xt, xb, c0, n)
        for k in range(KC):
            self.kb.stt("dve", out[:, k, oc0:oc0 + n], xt[:, k, c0:c0 + n], gvec[:, k:k + 1], rv, ALU.mult, ALU.mult,
                        [xb, rb, gb], [obuf])


def load_vec(kb, dram_vec, nchunk, name):
    t = kb.alloc([128, nchunk], F32)
    b = Buf(name)
    s = kb.new_sem(name)
    kb.dma(t, dram_vec.rearrange("(k p) -> p k", p=128), s, writes=[b], allow_slow_non_contiguous=True)
    return t, b


def load_weight_resident(kb, wst, dst, dbuf, w_ap, ncol, cw, ci0=0):
    ci = ci0
    for c in range(0, ncol, cw):
        load_cast(kb, wst, dst[:, :, c:c + cw], dbuf, w_ap[:, c:c + cw].rearrange("(k p) m -> p k m", p=128),
                  (slice(None), slice(0, dst.shape[1]), slice(0, cw)), ("act", "pool")[ci & 1])
        ci += 1


def phase_cast(kb, pairs, CW=2048):
    st = Ring(kb, "cst", 3, [128, CW], F32)
    bt = Ring(kb, "cbt", 3, [128, CW], BF16)
    ob = Buf("castout")
    ci = 0
    for src, dst in pairs:
        R, C = src.shape
        for n in range(R // 128):
            for c0 in range(0, C, CW):
                cw = min(CW, C - c0)
                s_, sb, ss = st.next()
                kb.dma(s_[:, 0:cw], src[n * 128:(n + 1) * 128, c0:c0 + cw], ss, writes=[sb])
                b_, bb, bs = bt.next()
                kb.copy(("act", "pool", "dve")[ci % 3], b_[:, 0:cw], s_[:, 0:cw], [sb], [bb])
                ci += 1
                kb.dma(dst[n * 128:(n + 1) * 128, c0:c0 + cw], b_[:, 0:cw], bs, reads=[bb], touch=[ob])


def phase_p3(kb, T, io, final):
    TB = 512 if T >= 512 else T
    consts = make_consts(kb)
    g3, g3b = load_vec(kb, io["g3"], KC, "g3")
    if final:
        gF, gFb = load_vec(kb, io["gF"], KC, "gF")
    cw = kb.alloc([128, 3, FC], F32)
    cwb = Buf("cw")
    cws = kb.new_sem("cw")
    for i in range(3):
        kb.dma(cw[:, i, :], io["conv_w"][i, :].rearrange("(k p) -> p k", p=128), cws, writes=[cwb], allow_slow_non_contiguous=True)
    cbv, cbb = load_vec(kb, io["conv_b"], FC, "cb")
    ctx01 = kb.alloc([128, 1], F32)
    kb.dma(ctx01, io["ctx01"], cws, writes=[cwb])
    xs = Ring(kb, "xs", 1, [128, KC, TB + 2], F32)
    hT = kb.alloc([128, KC, TB + 2], BF16)
    hbuf = Buf("hT")
    aT = kb.alloc([128, FC, TB], BF16)
    abuf = Buf("aT")
    nrm = NormSB(kb, consts, TB, banks=(0,))
    wbf = Ring(kb, "wbf", 3, [128, KC, 512], BF16)
    gts = Ring(kb, "gt", 2, [128, TB + 2], F32, dma=False)
    t1s = Ring(kb, "t1", 2, [128, TB], F32, dma=False)
    t2s = Ring(kb, "t2", 2, [128, TB], F32, dma=False)
    wds = Ring(kb, "wd", 3, [128, 4, 512], BF16)
    outr = Ring(kb, "outr", 2, [128, TB], F32)
    pr = PsRing(kb, [1, 2, 3, 4, 5, 6])
    phalo = kb.psum[:, 7, :]
    phb = kb.pbank[7]
    obuf = Buf("p3out")
    NG = (FC + 3) // 4
    for b in range(T // TB):
        t0 = b * TB
        xt, xb, xsem = xs.next()
        if b == 0:
            kb.dma(xt[:, :, 0:2], io["halo"].rearrange("(k p) t -> p k t", p=128), xsem, writes=[xb], allow_slow_non_contiguous=True)
            kb.dma(xt[:, :, 2:TB + 2], io["x2T"][:, 0:TB].rearrange("(k p) t -> p k t", p=128), xsem, writes=[xb])
            kb.ts("pool", xt[:, :, 0:2], xt[:, :, 0:2], ctx01[:, 0:1], None, ALU.mult, None, [xb, cwb], [xb])
        else:
            kb.dma(xt, io["x2T"][:, t0 - 2:t0 + TB].rearrange("(k p) t -> p k t", p=128), xsem, writes=[xb])
        nrm.apply(xt, xb, 2, TB, g3, g3b, hT, hbuf, 2)
        nrm.apply(xt, xb, 0, 2, g3, g3b, hT, hbuf, 0)
        for g in range(NG):
            c0 = g * 512
            ncol = min(512, DFF - c0)
            wv, wvb, wvs = wbf.next()
            kb.dma(wv[:, :, 0:ncol], io["w_up"][:, c0:c0 + ncol].rearrange("(k p) m -> p k m", p=128), wvs, writes=[wvb])
            wg, wgb, wgs = wbf.next()
            kb.dma(wg[:, :, 0:ncol], io["w_up"][:, DFF + c0:DFF + c0 + ncol].rearrange("(k p) m -> p k m", p=128), wgs, writes=[wgb])
            for jj in range(ncol // 128):
                j = 4 * g + jj
                cs = slice(jj * 128, (jj + 1) * 128)
                pv, pvb = pr.next()
                for k in range(KC):
                    kb.mm(pv[:, 0:TB], wv[:, k, cs], hT[:, k, 2:TB + 2], k == 0, k == KC - 1, [wvb, hbuf], [pvb])
                pg, pgb = pr.next()
                for k in range(KC):
                    kb.mm(pg[:, 0:TB], wg[:, k, cs], hT[:, k, 2:TB + 2], k == 0, k == KC - 1, [wgb, hbuf], [pgb])
                for k in range(KC):
                    kb.mm(phalo[:, 0:2], wg[:, k, cs], hT[:, k, 0:2], k == 0, k == KC - 1, [wgb, hbuf], [phb])
                gt, gtb, _ = gts.next()
                kb.copy("act", gt[:, 2:TB + 2], pg[:, 0:TB], [pgb], [gtb])
                kb.copy("dve", gt[:, 0:2], phalo[:, 0:2], [phb], [gtb])
                t1, t1b, _ = t1s.next()
                kb.ts("dve", t1, gt[:, 0:TB], cw[:, 0, j:j + 1], cbv[:, j:j + 1], ALU.mult, ALU.add, [gtb, cwb, cbb], [t1b])
                kb.stt("dve", t1, gt[:, 1:TB + 1], cw[:, 1, j:j + 1], t1, ALU.mult, ALU.add, [gtb, cwb, t1b], [t1b])
                kb.stt("dve", t1, gt[:, 2:TB + 2], cw[:, 2, j:j + 1], t1, ALU.mult, ALU.add, [gtb, cwb, t1b], [t1b])
                t2, t2b, _ = t2s.next()
                kb.act(t2, t1, AF.Silu, [t1b], [t2b])
                kb.tt("dve", aT[:, j, :], t2, pv[:, 0:TB], ALU.mult, [t2b, pvb], [abuf])
        for ig in range(KC // 4):
            banks = [1, 2, 3, 4]
            for g in range(NG):
                nj = min(4, FC - 4 * g)
                wd, wdb, wdsem = wds.next()
                kb.dma(wd[:, 0:nj, :], io["w_down"][g * 512:g * 512 + nj * 128, ig * 512:(ig + 1) * 512].rearrange("(j p) m -> p j m", p=128),
                       wdsem, writes=[wdb])
                for jj in range(nj):
                    j = 4 * g + jj
                    for i4 in range(4):
                        kb.mm(kb.psum[:, banks[i4], 0:TB], wd[:, jj, i4 * 128:(i4 + 1) * 128], aT[:, j, :], j == 0, j == FC - 1,
                              [wdb, abuf], [kb.pbank[banks[i4]]])
            for i4 in range(4):
                i = ig * 4 + i4
                po, pob = kb.psum[:, banks[i4], :], kb.pbank[banks[i4]]
                if not final:
                    ot, ob, os_ = outr.next()
                    kb.tt("dve", ot, po[:, 0:TB], xt[:, i, 2:TB + 2], ALU.add, [pob, xb], [ob])
                    kb.dma(io["yT"][i * 128:(i + 1) * 128, t0:t0 + TB], ot, os_, reads=[ob], touch=[obuf])
                else:
                    kb.tt("dve", xt[:, i, 2:TB + 2], po[:, 0:TB], xt[:, i, 2:TB + 2], ALU.add, [pob, xb], [xb])
        if final:
            rv, rb = nrm.rstd(xt, xb, 2, TB)
            for i in range(KC):
                ot, ob, os_ = outr.next()
                kb.stt("dve", ot, xt[:, i, 2:TB + 2], gF[:, i:i + 1], rv, ALU.mult, ALU.mult, [xb, rb, gFb], [ob])
                kb.dma(io["yT"][i * 128:(i + 1) * 128, t0:t0 + TB], ot, os_, reads=[ob], touch=[obuf])
    return obuf


def phase_p2b(kb, T, io, last_only=False):
    TB = 256
    consts = make_consts(kb)
    g2, g2b = load_vec(kb, io["g2"], KC, "g2")
    gM, gMb = load_vec(kb, io["gM"], KC, "gM")
    wsem = kb.new_sem("p2bw")
    wout = kb.alloc([128, KC, D], BF16); woutb = Buf("wout")
    wcq = kb.alloc([128, KC, X_W], BF16); wcqb = Buf("wcq")
    wco = kb.alloc([128, X_H, D], BF16); wcob = Buf("wco")
    kxT = kb.alloc([128, X_H, MEM], BF16); kxb = Buf("kxT")
    vx = kb.alloc([128, 2, X_W], BF16); vxb = Buf("vx")
    pr = PsRing(kb, [1, 2, 3, 4, 5, 6, 7])
    nrm = NormSB(kb, consts, TB, banks=(0,))
    xs = Ring(kb, "xs", 1, [128, KC, TB], F32)
    mts = Ring(kb, "mt", 2, [128, KC, TB], BF16)
    h2T = kb.alloc([128, KC, TB], BF16); h2b = Buf("h2T")
    q2T = kb.alloc([128, X_H, TB], BF16); q2b = Buf("q2T")
    PTs = Ring(kb, "PT", 4, [128, 2, TB], BF16, dma=False)
    rds = Ring(kb, "rd", 2, [128, TB], F32, dma=False)
    o2T = kb.alloc([128, X_H, TB], BF16); o2b = Buf("o2T")
    outs = kb.new_sem("x2out"); obuf = Buf("x2out")
    xt, xb, xsem = xs.next()
    kb.dma(xt[:, :, 0:MEM], io["memT"].rearrange("(k p) t -> p k t", p=128), xsem, writes=[xb])
    nrm.apply(xt, xb, 0, MEM, gM, gMb, h2T, h2b, 0)
    wkv = wout[:, :, 0:1024]
    for c in range(0, 1024, 512):
        kb.dma(wkv[:, :, c:c + 512], io["w_ckv"][:, c:c + 512].rearrange("(k p) m -> p k m", p=128), wsem, writes=[woutb])
    for h in range(X_H):
        ps, pb = pr.next()
        for k in range(KC):
            kb.mm(ps[:, 0:MEM], wkv[:, k, h * 128:(h + 1) * 128], h2T[:, k, 0:MEM], k == 0, k == KC - 1, [woutb, h2b], [pb])
        kb.copy(("dve", "act")[h & 1], kxT[:, h, :], ps[:, 0:MEM], [pb], [kxb])
    for mt in range(2):
        ps, pb = pr.next()
        for k in range(KC):
            kb.mm(ps[:, 0:X_W], h2T[:, k, mt * 128:(mt + 1) * 128], wkv[:, k, 512:1024], k == 0, k == KC - 1, [woutb, h2b], [pb])
        kb.copy(("dve", "act")[mt & 1], vx[:, mt, :], ps[:, 0:X_W], [pb], [vxb])
    for c in range(0, D, 512):
        kb.dma(wout[:, :, c:c + 512], io["w_out"][:, c:c + 512].rearrange("(k p) m -> p k m", p=128), wsem, writes=[woutb])
    kb.dma(wcq, io["w_cq"].rearrange("(k p) m -> p k m", p=128), wsem, writes=[wcqb])
    for c in range(0, D, 512):
        kb.dma(wco[:, :, c:c + 512], io["w_co"][:, c:c + 512].rearrange("(k p) m -> p k m", p=128), wsem, writes=[wcob])
    scale = float(X_DH) ** -0.5
    for b in range(T // TB):
        if last_only and b != T // TB - 1:
            continue
        t0 = b * TB
        xt, xb, xsem = xs.next()
        kb.dma(xt, io["xT"][:, t0:t0 + TB].rearrange("(k p) t -> p k t", p=128), xsem, writes=[xb])
        mt_, mb, msem = mts.next()
        kb.dma(mt_, io["mixT"][:, t0:t0 + TB].rearrange("(k p) t -> p k t", p=128), msem, writes=[mb])
        for i in range(KC):
            ps, pb = pr.next()
            for k in range(KC):
                kb.mm(ps[:, 0:TB], wout[:, k, i * 128:(i + 1) * 128], mt_[:, k, :], k == 0, k == KC - 1, [woutb, mb], [pb])
            kb.tt("dve", xt[:, i, :], ps[:, 0:TB], xt[:, i, :], ALU.add, [pb, xb], [xb])
        nrm.apply(xt, xb, 0, TB, g2, g2b, h2T, h2b, 0)
        for h in range(X_H):
            ps, pb = pr.next()
            for k in range(KC):
                kb.mm(ps[:, 0:TB], wcq[:, k, h * 128:(h + 1) * 128], h2T[:, k, :], k == 0, k == KC - 1, [wcqb, h2b], [pb])
            kb.copy(("dve", "act")[h & 1], q2T[:, h, :], ps[:, 0:TB], [pb], [q2b])
        PTl = []
        for h in range(X_H):
            PT, PTb, _ = PTs.next()
            ps, pb = pr.next()
            for mt in range(2):
                kb.mm(ps[:, mt * TB:(mt + 1) * TB], kxT[:, h, mt * 128:(mt + 1) * 128], q2T[:, h, :], True, True, [kxb, q2b], [pb])
            kb.act(PT.rearrange("p m t -> p (m t)"), ps[:, 0:2 * TB], AF.Exp, [pb], [PTb], scale=scale)
            PTl.append((PT, PTb))
        for h in range(X_H):
            PT, PTb = PTl[h]
            po, pob = pr.next()
            for mt in range(2):
                kb.mm(po[:, 0:TB], vx[:, mt, h * 128:(h + 1) * 128], PT[:, mt, :], mt == 0, mt == 1, [vxb, PTb], [pob])
            pd, pdb = pr.next()
            for mt in range(2):
                kb.mm(pd[:, 0:TB], consts["ones_bf"], PT[:, mt, :], mt == 0, mt == 1, [consts["buf"], PTb], [pdb])
            rd, rdb, _ = rds.next()
            kb.recip(rd, pd[:, 0:TB], [pdb], [rdb])
            kb.tt("dve", o2T[:, h, :], po[:, 0:TB], rd, ALU.mult, [pob, rdb], [o2b])
        for i in range(KC):
            ps, pb = pr.next()
            for h in range(X_H):
                kb.mm(ps[:, 0:TB], wco[:, h, i * 128:(i + 1) * 128], o2T[:, h, :], h == 0, h == X_H - 1, [wcob, o2b], [pb])
            kb.tt("dve", xt[:, i, :], ps[:, 0:TB], xt[:, i, :], ALU.add, [pb, xb], [xb])
        kb.dma(io["x2T"][:, t0:t0 + TB].rearrange("(k p) t -> p k t", p=128), xt, outs, reads=[xb], touch=[obuf])
    return obuf


def phase_fox(kb, T, io, noctx=False, last_only=False):
    NT = 2 * T // 128
    NCT = T // 128
    QB = 512
    consts = make_consts(kb)
    cb_ = consts["buf"]
    ones32 = kb.alloc([128, 128], F32)
    triU = kb.alloc([128, 128], F32)
    kb.memset("dve", ones32, 1.0, [cb_])
    kb.op("pool", lambda e: e.affine_select(out=triU, in_=ones32, pattern=[[1, 128]], compare_op=ALU.is_ge, fill=0.0,
                                            base=0, channel_multiplier=-1), [cb_], [cb_])
    mask = kb.alloc([128, 4, QB], BF16)
    msem = kb.new_sem("maskB")
    kb.dma(mask, io["maskB"].rearrange("k p c -> p k c"), msem, writes=[cb_])
    ctxneg = kb.alloc([128, 1], F32)
    kb.dma(ctxneg, io["ctxneg"], msem, writes=[cb_])
    fl = kb.alloc([128, NT, B_H], F32); flb = Buf("fl")
    fsem = kb.new_sem("flsem")
    kb.dma(fl[:, 0:NCT, :], io["fl_ctx"].rearrange("(n p) h -> p n h", p=128), fsem, writes=[flb])
    kb.dma(fl[:, NCT:NT, :], io["fl"].rearrange("(n p) h -> p n h", p=128), fsem, writes=[flb])
    fbt = kb.alloc([128, B_H], F32)
    kb.dma(fbt, io["fb"].partition_broadcast(128), fsem, writes=[flb])
    for n in range(NT):
        kb.tt("dve" if n % 2 else "pool", fl[:, n, :], fl[:, n, :], fbt, ALU.add, [flb], [flb])
    flf = fl.rearrange("p n h -> p (n h)")
    kb.act(flf, flf, AF.Exp, [flb], [flb], scale=-1.0)
    kb.act(flf, flf, AF.Ln, [flb], [flb], bias=1.0)
    cp = kb.alloc([128, NT, B_H], F32); cpb = Buf("cp")
    tot = kb.alloc([128, NT, B_H], F32)
    NF = NT * B_H
    for c0 in range(0, NF, 512):
        c1 = min(NF, c0 + 512)
        ps, pb = kb.psum[:, 0, :], kb.pbank[0]
        kb.mm(ps[:, 0:c1 - c0], triU, flf[:, c0:c1], True, True, [flb, cb_], [pb])
        kb.copy("dve", cp.rearrange("p n h -> p (n h)")[:, c0:c1], ps[:, 0:c1 - c0], [pb], [cpb])
        ps, pb = kb.psum[:, 1, :], kb.pbank[1]
        kb.mm(ps[:, 0:c1 - c0], ones32, flf[:, c0:c1], True, True, [flb, cb_], [pb])
        kb.copy("dve", tot.rearrange("p n h -> p (n h)")[:, c0:c1], ps[:, 0:c1 - c0], [pb], [cpb])
    carry = kb.alloc([128, NT + 1, B_H], F32)
    kb.memset("dve", carry[:, 0, :], 0.0, [cpb])
    for n in range(NT):
        kb.tt("dve", carry[:, n + 1, :], carry[:, n, :], tot[:, n, :], ALU.add, [cpb], [cpb])
        kb.tt("pool", cp[:, n, :], cp[:, n, :], carry[:, n, :], ALU.add, [cpb], [cpb])
    for n in range(NCT):
        kb.ts("pool", cp[:, n, :], cp[:, n, :], ctxneg[:, 0:1], None, ALU.add, None, [cpb, cb_], [cpb])
    qs = Ring(kb, "fq", 2, [64, T], BF16)
    ks = Ring(kb, "fk", 2, [64, 2 * T], BF16)
    vs = Ring(kb, "fv", 2, [128, NT, 128], BF16)
    for t_ in vs.t:
        kb.memset("pool", t_[:, :, 64:128], 1.0, [vs.b[vs.t.index(t_)]])
    biasr = Ring(kb, "fbias", 2, [128, T // QB, NT], F32, dma=False)
    PTs = Ring(kb, "fP", 4, [128, QB], BF16, dma=False)
    rds = Ring(kb, "frd", 2, [64, QB], F32, dma=False)
    outs = Ring(kb, "fout", 2, [64, T], BF16)
    prS = PsRing(kb, [2, 3, 4, 5])
    prO = PsRing(kb, [6, 7])
    obuf = Buf("foxout")
    LA = 2
    tiles = []
    heads = {}
    for h in range(B_H):
        for j in range(T // QB):
            if last_only and j != T // QB - 1:
                continue
            nk = NCT + 4 * j + 4
            n0 = NCT if noctx else 0
            for n in range(n0, nk):
                tiles.append((h, j, n, n == n0, n == nk - 1))
    state = {}

    def load_head(h):
        qt, qb, qsem = qs.next()
        kb.dma(qt, io["qBT"][h * 64:(h + 1) * 64, :], qsem, writes=[qb])
        kt, kbf, ksem = ks.next()
        kb.dma(kt[:, 0:T], io["kBT_ctx"][h * 64:(h + 1) * 64, :], ksem, writes=[kbf])
        kb.dma(kt[:, T:2 * T], io["kBT"][h * 64:(h + 1) * 64, :], ksem, writes=[kbf])
        vt, vb, vsem = vs.next()
        kb.dma(vt[:, 0:NCT, 0:64], io["vB_ctx"][:, h * 64:(h + 1) * 64].rearrange("(n p) e -> p n e", p=128), vsem, writes=[vb])
        kb.dma(vt[:, NCT:NT, 0:64], io["vB"][:, h * 64:(h + 1) * 64].rearrange("(n p) e -> p n e", p=128), vsem, writes=[vb])
        bias_, biasb_, _ = biasr.next()
        for j in range(T // QB):
            kb.ts("dve", bias_[:, j, :], cp[:, :, h], carry[:, NCT + 4 * j, h:h + 1], None, ALU.subtract, None, [cpb], [biasb_])
        ot, ob, osem = outs.next()
        heads[h] = (qt, qb, kt, kbf, vt, vb, bias_, biasb_, ot, ob, osem)

    def stage1(i):
        h, j, n, first, last = tiles[i]
        if h not in heads:
            load_head(h)
        qt, qb, kt, kbf, vt, vb, bias_, biasb_, ot, ob, osem = heads[h]
        ps, pb = prS.next()
        kb.mm(ps, kt[:, n * 128:(n + 1) * 128], qt[:, j * QB:(j + 1) * QB], True, True, [kbf, qb], [pb])
        PT, PTb, _ = PTs.next()
        kb.act(PT, ps, AF.Exp, [pb, biasb_], [PTb], scale=0.125, bias=bias_[:, j, n:n + 1])
        d = n - (NCT + 4 * j)
        if d >= 0:
            kb.tt("dve", PT, PT, mask[:, d, :], ALU.mult, [PTb, cb_], [PTb])
        state[i] = (PT, PTb)

    def stage2(i):
        h, j, n, first, last = tiles[i]
        qt, qb, kt, kbf, vt, vb, bias_, biasb_, ot, ob, osem = heads[h]
        PT, PTb = state.pop(i)
        if first:
            state["po"] = prO.next()
        po, pob = state["po"]
        kb.mm(po, vt[:, n, :], PT, first, last, [vb, PTb], [pob])
        if last:
            rd, rdb, _ = rds.next()
            kb.recip(rd, po[64:128, :], [pob], [rdb])
            kb.tt("dve", ot[:, j * QB:(j + 1) * QB], po[0:64, :], rd, ALU.mult, [pob, rdb], [ob])
            if j == T // QB - 1:
                kb.dma(io["mixT"][A_W + h * 64:A_W + (h + 1) * 64, :], ot, osem, reads=[ob], touch=[obuf])

    N = len(tiles)
    for i in range(N + LA):
        if i < N:
            stage1(i)
        if i >= LA:
            stage2(i - LA)
    return obuf


A_PAT = (1, 4, 16)


def phase_dil(kb, nc, T, io, uid, last_only=False):
    consts = make_consts(kb)
    cb_ = consts["buf"]
    ctx01 = kb.alloc([128, 1], F32)
    csem = kb.new_sem("dconst")
    kb.dma(ctx01, io["ctx01"], csem, writes=[cb_])
    rel = kb.alloc([32, A_H], F32)
    kb.dma(rel, io["rel"], csem, writes=[cb_])
    oh = kb.alloc([32, 3, 129], F32)
    kb.dma(oh, io["OH"].rearrange("q b m -> b q m"), csem, writes=[cb_])
    ones32 = kb.alloc([128, 128], F32)
    J = kb.alloc([128, 128], F32)
    kb.memset("dve", ones32, 1.0, [cb_])
    kb.op("pool", lambda e: e.affine_select(out=J, in_=ones32, pattern=[[-1, 128]], compare_op=ALU.is_equal, fill=0.0,
                                            base=127, channel_multiplier=-1), [cb_], [cb_])
    gsc_t = nc.dram_tensor("gsc%d" % uid, [3 * A_H * 2 * 256], F32)
    gsc = gsc_t.ap()
    gsem = kb.new_sem("gsc"); gscb = Buf("gsc")
    fe = kb.alloc([A_H, 129], F32); feb = Buf("fe")
    G = kb.alloc([A_H, 2, 256], F32); Gb = Buf("G")
    EB32 = kb.alloc([128, A_H, 256], F32); EB32b = Buf("EB32"); ebsem = kb.new_sem("eb32")
    EB = [kb.alloc([128, A_H, 256], BF16) for _ in range(3)]
    EB0 = [kb.alloc([128, A_H, 256], BF16) for _ in range(3)]
    EBb = Buf("EB")
    prE = PsRing(kb, [2, 3, 4, 5])
    for p in range(3):
        ps, pb = kb.psum[:, 0, :], kb.pbank[0]
        kb.mm(ps[0:A_H, 0:129], rel, oh[:, p, :], True, True, [cb_], [pb])
        kb.act(fe, ps[0:A_H, 0:129], AF.Exp, [pb], [feb])
        kb.memset("dve", G.rearrange("h w m -> h (w m)"), 0.0, [Gb])
        kb.copy("dve", G[:, 0, 0:128], fe[:, 1:129], [feb], [Gb])
        kb.copy("dve", G[:, 1, 127:256], fe[:, 0:129], [feb], [Gb])
        kb.dma(gsc[p * A_H * 512:(p + 1) * A_H * 512].rearrange("(h m) -> h m", h=A_H), G.rearrange("h w m -> h (w m)"),
               gsem, reads=[Gb], writes=[gscb])
        for h in range(A_H):
            for w in range(2):
                src = bass.AP(gsc_t, (p * A_H + h) * 512 + w * 256, [[1, 128], [1, 128]])
                kb.dma(EB32[:, h, w * 128:(w + 1) * 128], src, ebsem, reads=[gscb], writes=[EB32b])
        for h in range(A_H):
            ps, pb = prE.next()
            kb.mm(ps[:, 0:256], J, EB32[:, h, :], True, True, [cb_, EB32b], [pb])
            kb.copy("act", EB[p][:, h, :], ps[:, 0:256], [pb], [EBb])
            kb.copy("act", EB0[p][:, h, 128:256], ps[:, 128:256], [pb], [EBb])
            kb.ts("dve", EB0[p][:, h, 0:128], ps[:, 0:128], ctx01[:, 0:1], None, ALU.mult, None, [pb, cb_], [EBb])
    CA = min(2048, T)
    KOFF = 2048
    qs = Ring(kb, "dq", 2, [64, T], BF16)
    ks = Ring(kb, "dk", 2, [64, KOFF + T], BF16)
    NVT = (T + 2048) // 128
    vs = Ring(kb, "dv", 2, [128, NVT, 128], BF16)
    for i_, t_ in enumerate(vs.t):
        kb.memset("pool", t_[:, :, 64:128], 1.0, [vs.b[i_]])
    accs = Ring(kb, "dacc", 2, [128, T], F32, dma=False)
    PTs = Ring(kb, "dP", 4, [128, 256], BF16, dma=False)
    rds = Ring(kb, "drd", 1, [64, T], F32, dma=False)
    outs = Ring(kb, "dout", 2, [64, T], BF16)
    prS = PsRing(kb, [2, 3, 4, 5])
    prO = PsRing(kb, [6, 7, 0, 1])
    obuf = Buf("dilout")
    LA = 2
    tiles = []
    for h in range(A_H):
        for p, d in enumerate(A_PAT):
            NU = (T // d) // 128
            for r in range(d):
                for u in range(NU):
                    if last_only and u != NU - 1:
                        continue
                    tiles.append((h, p, d, r, u))
    last_of_head = {}
    for i, t_ in enumerate(tiles):
        last_of_head[t_[0]] = i
    hd, hv, state = {}, {}, {}

    def load_head(h):
        hr = slice(h * 64, (h + 1) * 64)
        acc, accb0, _ = accs.next()
        if not hasattr(accs, "fine"):
            accs.fine = {}
        accb = accs.fine.setdefault(id(accb0), [Buf("accf%d" % i_) for i_ in range(T // 128)])
        qt, qb, qsem = qs.next()
        kb.dma(qt, io["qAT"][hr, :], qsem, writes=[qb])
        kt, kbf, ksem = ks.next()
        kb.dma(kt[:, KOFF - CA:KOFF], io["kAT_ctx"][hr, T - CA:T], ksem, writes=[kbf])
        kb.dma(kt[:, KOFF:KOFF + T], io["kAT"][hr, :], ksem, writes=[kbf])
        hd[h] = (acc, accb, qt, qb, kt, kbf)

    def load_v(h, p, d):
        hr = slice(h * 64, (h + 1) * 64)
        NU = (T // d) // 128
        vt, vb, vsem = vs.next()
        for r in range(d):
            kb.dma(vt[:, r * (1 + NU), 0:64],
                   io["vA_ctx"][T - 128 * d:T, hr].rearrange("(p d) e -> d p e", d=d)[r], vsem, writes=[vb])
            kb.dma(vt[:, r * (1 + NU) + 1:(r + 1) * (1 + NU), 0:64],
                   io["vA"][:, hr].rearrange("(n p d) e -> d p n e", p=128, d=d)[r], vsem, writes=[vb])
        hv[(h, p)] = (vt, vb)

    def stage1(i):
        h, p, d, r, u = tiles[i]
        if h not in hd:
            load_head(h)
        if (h, p) not in hv:
            load_v(h, p, d)
        acc, accb, qt, qb, kt, kbf = hd[h]
        q0 = r + 128 * u * d
        qv = qt[:, q0:q0 + 127 * d + 1:d]
        k0 = KOFF - 128 * d + r + 128 * u * d
        k1 = k0 + 128 * d
        ps, pb = prS.next()
        kb.mm(ps[:, 0:128], kt[:, k0:k0 + 127 * d + 1:d], qv, True, True, [kbf, qb], [pb])
        kb.mm(ps[:, 128:256], kt[:, k1:k1 + 127 * d + 1:d], qv, True, True, [kbf, qb], [pb])
        PT, PTb, _ = PTs.next()
        kb.act(PT, ps[:, 0:256], AF.Exp, [pb], [PTb], scale=0.125)
        kb.tt("dve" if i % 2 == 0 else "pool", PT, PT, (EB0 if u == 0 else EB)[p][:, h, :], ALU.mult, [PTb, EBb], [PTb])
        state[i] = (PT, PTb)

    def stage2(i):
        h, p, d, r, u = tiles[i]
        acc, accb, qt, qb, kt, kbf = hd[h]
        vt, vb = hv[(h, p)]
        NU = (T // d) // 128
        PT, PTb = state.pop(i)
        po, pob = prO.next()
        kb.mm(po[:, 0:128], vt[:, r * (1 + NU) + u, :], PT[:, 0:128], True, False, [vb, PTb], [pob])
        kb.mm(po[:, 0:128], vt[:, r * (1 + NU) + u + 1, :], PT[:, 128:256], False, True, [vb, PTb], [pob])
        q0 = r + 128 * u * d
        av = acc[:, q0:q0 + 127 * d + 1:d]
        abs_ = accb[u * d:(u + 1) * d]
        if p == 0:
            kb.copy("act", av, po[:, 0:128], [pob], abs_)
        else:
            kb.tt("dve", av, po[:, 0:128], av, ALU.add, [pob] + abs_, abs_)
        if last_of_head[h] == i:
            rd, rdb, _ = rds.next()
            kb.recip(rd, acc[64:128, :], accb, [rdb])
            ot, ob, osem = outs.next()
            kb.tt("dve", ot, acc[0:64, :], rd, ALU.mult, accb + [rdb], [ob])
            kb.dma(io["mixT"][h * 64:(h + 1) * 64, :], ot, osem, reads=[ob], touch=[obuf])

    N = len(tiles)
    for i in range(N + LA):
        if i < N:
            stage1(i)
        if i >= LA:
            stage2(i - LA)
    return obuf


def t5_bucket_np(dist):
    max_exact = 16
    dd = np.maximum(dist, 1).astype(np.float32)
    large = max_exact + (np.log(dd / max_exact) / math.log(2048 / max_exact) * (32 - max_exact)).astype(np.int32)
    return np.where(dist < max_exact, dist, np.minimum(large, 31)).astype(np.int32)


def dil_onehot():
    oh = np.zeros((3, 32, 129), np.float32)
    for p, d in enumerate(A_PAT):
        b = t5_bucket_np(np.arange(129) * d)
        oh[p, b, np.arange(129)] = 1.0
    return oh


def phase_gla(kb, T, io, noctx=False, last_only=False):
    NCT = T // 128
    NT = 2 * NCT
    consts = make_consts(kb)
    cb_ = consts["buf"]
    csem = kb.new_sem("gconst")
    ones32 = kb.alloc([128, 128], F32)
    kb.memset("dve", ones32, 1.0 / 16.0, [cb_])
    triUs = kb.alloc([128, 128], F32)
    triLs = kb.alloc([128, 128], F32)
    triU1 = kb.alloc([128, 128], F32)
    kb.op("pool", lambda e: e.affine_select(out=triUs, in_=ones32, pattern=[[1, 128]], compare_op=ALU.is_ge, fill=0.0,
                                            base=0, channel_multiplier=-1), [cb_], [cb_])
    kb.tt("dve", triLs, ones32, triUs, ALU.subtract, [cb_], [cb_])
    kb.ts("dve", triU1, triUs, 16.0, None, ALU.mult, None, [cb_], [cb_])
    ctx01 = kb.alloc([128, 1], F32)
    kb.dma(ctx01, io["ctx01"], csem, writes=[cb_])
    w2s = kb.alloc([16, C_KW], F32)
    kb.dma(w2s, io["w2"], csem, writes=[cb_])
    w2b = kb.alloc([16, C_KW], BF16)
    kb.copy("dve", w2b, w2s, [cb_], [cb_])
    cgb = kb.alloc([128, C_KW], F32)
    kb.dma(cgb, io["cgb"].partition_broadcast(128), csem, writes=[cb_])
    cn = kb.alloc([128, C_DV], F32)
    kb.dma(cn, io["cn"].partition_broadcast(128), csem, writes=[cb_])
    gls = kb.alloc([16, 2 * T], F32)
    kb.dma(gls[:, 0:T], io["glT_ctx"], csem, writes=[cb_])
    kb.dma(gls[:, T:2 * T], io["glT"], csem, writes=[cb_])
    ident = kb.alloc([128, 128], BF16)
    onesb = kb.alloc([128, 128], BF16)
    kb.memset("dve", onesb, 1.0, [cb_])
    kb.op("pool", lambda e: e.affine_select(out=ident, in_=onesb, pattern=[[-1, 128]], compare_op=ALU.is_equal, fill=0.0,
                                            base=0, channel_multiplier=1), [cb_], [cb_])
    ocTs = Ring(kb, "gocT", 2, [128, 6, 128], BF16)
    glb = kb.alloc([16, 2 * T], BF16)
    kb.copy("pool", glb, gls, [cb_], [cb_])
    S = kb.alloc([96, C_H, C_DV], F32); Sb = Buf("S")
    Sbf = kb.alloc([96, C_H, C_DV], BF16); Sbfb = Buf("Sbf")
    kb.memset("dve", S.rearrange("k h v -> k (h v)"), 0.0, [Sb])
    kb.memset("pool", Sbf.rearrange("k h v -> k (h v)"), 0.0, [Sbfb])
    kin = Ring(kb, "gk", 2, [128, C_H, C_DK], BF16)
    vin = Ring(kb, "gv", 2, [128, C_H, C_DV], BF16)
    qTin = Ring(kb, "gqT", 2, [96, C_H, 128], BF16)
    kTin = Ring(kb, "gkT", 2, [96, C_H, 128], BF16)
    rin = Ring(kb, "gr", 2, [128, C_VW], BF16)
    Gs = Ring(kb, "gG", 2, [128, C_KW], F32, dma=False)
    E3s = Ring(kb, "gE3", 2, [128, C_KW], F32, dma=False)
    khs = Ring(kb, "gkh", 2, [128, C_H, C_DK], BF16, dma=False)
    decs = Ring(kb, "gdec", 2, [96, C_H], F32, dma=False)
    E1s = Ring(kb, "gE1", 2, [96, 128], F32, dma=False)
    qts = Ring(kb, "gqt", 2, [96, 128], BF16, dma=False)
    kts = Ring(kb, "gkt", 2, [96, 128], BF16, dma=False)
    Ams = Ring(kb, "gAm", 2, [128, 128], BF16, dma=False)
    sqs = Ring(kb, "gsq", 2, [128, C_DV], F32, dma=False)
    sss = Ring(kb, "gss", 4, [128, 1], F32, dma=False)
    sils = Ring(kb, "gsil", 2, [128, C_VW], F32, dma=False)
    ocs = Ring(kb, "goc", 2, [128, C_VW], BF16)
    pr = PsRing(kb, [0, 1, 2, 3, 4, 5, 6, 7])
    obuf = Buf("glaout")
    for n in range(NCT if noctx else 0, NT):
        own = n >= NCT
        emit = own and not (last_only and n != NT - 1)
        ps, pb = pr.next()
        kb.mm(ps[:, 0:C_KW], glb[:, n * 128:(n + 1) * 128], w2b, True, True, [cb_], [pb])
        G, Gb, _ = Gs.next()
        kb.tt("dve", G, ps[:, 0:C_KW], cgb, ALU.add, [pb, cb_], [Gb])
        kb.act(G, G, AF.Exp, [Gb], [Gb], scale=-1.0)
        kb.act(G, G, AF.Ln, [Gb], [Gb], bias=1.0)
        kt, ktb, ksem = kin.next()
        ksrc = io["kC_ctx"][n * 128:(n + 1) * 128] if n < NCT else io["kC"][(n - NCT) * 128:(n - NCT + 1) * 128]
        kb.dma(kt, ksrc.rearrange("p (h k) -> p h k", h=C_H), ksem, writes=[ktb])
        vt, vtb, vsem = vin.next()
        vsrc = io["vC_ctx"][n * 128:(n + 1) * 128] if n < NCT else io["vC"][(n - NCT) * 128:(n - NCT + 1) * 128]
        kb.dma(vt, vsrc.rearrange("p (h v) -> p h v", h=C_H), vsem, writes=[vtb])
        if not own:
            vf = vt.rearrange("p h v -> p (h v)")
            kb.ts("pool", vf, vf, ctx01[:, 0:1], None, ALU.mult, None, [vtb, cb_], [vtb])
        ps, pb = pr.next()
        kb.mm(ps[:, 0:C_KW], triLs, G, True, True, [Gb, cb_], [pb])
        E3, E3b, _ = E3s.next()
        kb.act(E3, ps[:, 0:C_KW], AF.Exp, [pb], [E3b], scale=-1.0)
        kh, khb, _ = khs.next()
        kb.tt("dve", kh.rearrange("p h k -> p (h k)"), kt.rearrange("p h k -> p (h k)"), E3, ALU.mult, [ktb, E3b], [khb])
        psd, pdb = pr.next()
        for h in range(C_H):
            kb.mm(psd[0:96, h:h + 1], G[:, h * 96:(h + 1) * 96], ones32[:, 0:1], True, True, [Gb, cb_], [pdb])
        dec, decb, _ = decs.next()
        kb.act(dec, psd[0:96, 0:C_H], AF.Exp, [pdb], [decb], scale=-1.0)
        if emit:
            c = n - NCT
            qT, qTb, qsem = qTin.next()
            kb.dma(qT, io["qCT"].rearrange("(h k) t -> k h t", h=C_H)[:, :, c * 128:(c + 1) * 128], qsem, writes=[qTb])
            kT, kTb, kTsem = kTin.next()
            kb.dma(kT, io["kCT"].rearrange("(h k) t -> k h t", h=C_H)[:, :, c * 128:(c + 1) * 128], kTsem, writes=[kTb])
            rt, rtb, rsem = rin.next()
            kb.dma(rt, io["rC"][c * 128:(c + 1) * 128, :], rsem, writes=[rtb])
            sil, silb, _ = sils.next()
            kb.act(sil, rt, AF.Silu, [rtb], [silb])
            oc, ocb, ocsem = ocs.next()
            for h in range(C_H):
                psb, pbb = pr.next()
                kb.mm(psb[0:96, 0:128], G[:, h * 96:(h + 1) * 96], triUs, True, True, [Gb, cb_], [pbb])
                E1, E1b, _ = E1s.next()
                kb.act(E1, psb[0:96, 0:128], AF.Exp, [pbb], [E1b], scale=-1.0)
                qt_, qtb_, _ = qts.next()
                kb.stt("dve", qt_, qT[:, h, :], float(C_DK) ** -0.5, E1, ALU.mult, ALU.mult, [qTb, E1b], [qtb_])
                E2, E2b, _ = E1s.next()
                kb.act(E2, psb[0:96, 0:128], AF.Exp, [pbb], [E2b])
                kt_, ktb_, _ = kts.next()
                kb.tt("dve", kt_, kT[:, h, :], E2, ALU.mult, [kTb, E2b], [ktb_])
                psa, pab = pr.next()
                kb.mm(psa[:, 0:128], kt_, qt_, True, True, [ktb_, qtb_], [pab])
                Am, Amb, _ = Ams.next()
                kb.tt("dve", Am, psa[:, 0:128], triU1, ALU.mult, [pab, cb_], [Amb])
                pso, pob = pr.next()
                kb.mm(pso[:, 0:C_DV], qt_, Sbf[:, h, :], True, False, [qtb_, Sbfb], [pob])
                kb.mm(pso[:, 0:C_DV], Am, vt[:, h, :], False, True, [Amb, vtb], [pob])
                sq, sqb, _ = sqs.next()
                kb.act(sq, pso[:, 0:C_DV], AF.Square, [pob], [sqb])
                ss, ssb, _ = sss.next()
                kb.op("dve", lambda e, ss=ss, sq=sq: e.reduce_sum(out=ss, in_=sq, axis=AX.X), [sqb], [ssb])
                kb.act(ss, ss, AF.Sqrt, [ssb, cb_], [ssb], scale=1.0 / C_DV, bias=consts["eps"])
                kb.recip(ss, ss, [ssb], [ssb])
                kb.stt("dve", sq, pso[:, 0:C_DV], ss[:, 0:1], cn, ALU.mult, ALU.mult, [pob, ssb, cb_], [sqb])
                kb.tt("dve", oc[:, h * C_DV:(h + 1) * C_DV], sq, sil[:, h * C_DV:(h + 1) * C_DV], ALU.mult, [sqb, silb], [ocb])
            ocT, ocTb, ocTsem = ocTs.next()
            for j in range(6):
                pst, ptb = pr.next()
                pv = pst.bitcast(BF16)[:, 0:128]
                kb.tr(pv, oc[:, j * 128:(j + 1) * 128], ident, [ocb, cb_], [ptb])
                kb.copy(("dve", "act")[j & 1], ocT[:, j, :], pv, [ptb], [ocTb])
            kb.dma(io["mixT"][A_W + B_W:D, c * 128:(c + 1) * 128].rearrange("(j p) t -> p j t", p=128), ocT, ocTsem,
                   reads=[ocTb], touch=[obuf])
        for h in range(C_H):
            pss, psb_ = pr.next()
            kb.mm(pss[0:96, 0:C_DV], kh[:, h, :], vt[:, h, :], True, True, [khb, vtb], [psb_])
            kb.stt("dve", S[:, h, :], S[:, h, :], dec[:, h:h + 1], pss[0:96, 0:C_DV], ALU.mult, ALU.add, [Sb, decb, psb_], [Sb])
        kb.copy("pool", Sbf.rearrange("k h v -> k (h v)"), S.rearrange("k h v -> k (h v)"), [Sb], [Sbfb])
    return obuf


P1_OUT = [(n, "fm", r * nb) for n, c0, r, nb in FM_GROUPS] + [(n, "tm", tot) for n, c0, tot, cw in TM_GROUPS]


def build_fused(T, depth=2):
    nc = bass.Bass("TRN2", target_bir_lowering=False)
    X = {}

    def inp(name, shape, dt=F32):
        X[name] = nc.dram_tensor(name, list(shape), dt, kind="ExternalInput").ap()

    def scr(name, shape, dt):
        return nc.dram_tensor(name, list(shape), dt).ap()

    inp("xTA", [D, T]); inp("xTB", [D, T]); inp("memT", [D, MEM])
    inp("ctx01A", [128, 1]); inp("ctx01B", [128, 1]); inp("ctxnegA", [128, 1]); inp("ctxnegB", [128, 1])
    inp("maskB", [4, 128, 512], BF16); inp("OH", [3, 32, 129]); inp("rel", [32, A_H])
    inp("norm_mix", [depth, D]); inp("w_in", [depth, D, PROJ_W]); inp("f_bias", [depth, B_H])
    inp("c_gate_w2", [depth, C_RANK, C_KW]); inp("c_gate_b", [depth, C_KW]); inp("c_norm", [depth, C_DV])
    inp("w_out", [depth, D, D]); inp("norm_cross", [depth, D]); inp("w_cq", [depth, D, X_W]); inp("w_ckv", [depth, D, 2 * X_W])
    inp("w_co", [depth, X_W, D]); inp("norm_ffn", [depth, D]); inp("w_up", [depth, D, 2 * DFF]); inp("conv_w", [depth, 3, DFF])
    inp("conv_b", [depth, DFF]); inp("w_down", [depth, DFF, D]); inp("mem_norm", [D]); inp("norm_final", [D])
    yT = nc.dram_tensor("yT", [D, T], F32, kind="ExternalOutput").ap()
    P = {}
    for ps_ in "AB":
        P[ps_] = {}
        for name, kind, n in P1_OUT:
            dt = F32 if name in ("glT", "fl") else BF16
            P[ps_][name] = scr("%s_%s" % (name, ps_), [n, T] if kind == "fm" else [T, n], dt)
    mixT = scr("mixT", [D, T], BF16)
    x2 = {"A": scr("x2A", [D, T], F32), "B": scr("x2B", [D, T], F32)}
    xn = {"A": scr("xnA", [D, T], F32), "B": scr("xnB", [D, T], F32)}
    uid = [0]
    WN = (("w_in", [D, PROJ_W]), ("w_out", [D, D]), ("w_cq", [D, X_W]), ("w_ckv", [D, 2 * X_W]), ("w_co", [X_W, D]),
          ("w_up", [D, 2 * DFF]), ("w_down", [DFF, D]))
    W = [{n: scr("%s_bf%d" % (n, l), shp, BF16) for n, shp in WN} for l in range(depth)]
    with contextlib.ExitStack() as st:
        kb = KB(nc, st)
        xcur = {"A": X["xTA"], "B": X["xTB"]}
        for l in range(depth):
            kb.phase()
            phase_cast(kb, [(X[n][l], W[l][n]) for n, shp in WN])
            for ps_ in "AB":
                if ps_ == "A" and False:
                    continue
                own, ctx = P[ps_], P["A"]
                isA = ps_ == "A"
                lastA = isA and l == depth - 1
                c01, cneg = X["ctx01" + ps_], X["ctxneg" + ps_]
                kb.phase()
                io = {"xT": xcur[ps_], "g1": X["norm_mix"][l], "w_in": W[l]["w_in"]}
                io.update(own)
                phase_p1(kb, T, io)
                kb.phase()
                phase_fox(kb, T, {"qBT": own["qBT"], "kBT": own["kBT"], "kBT_ctx": ctx["kBT"], "vB": own["vB"], "vB_ctx": ctx["vB"],
                                  "fl": own["fl"], "fl_ctx": ctx["fl"], "fb": X["f_bias"][l], "ctxneg": cneg, "maskB": X["maskB"],
                                  "mixT": mixT}, noctx=isA, last_only=lastA)
                kb.phase()
                uid[0] += 1
                phase_dil(kb, nc, T, {"qAT": own["qAT"], "kAT": own["kAT"], "kAT_ctx": ctx["kAT"], "vA": own["vA"], "vA_ctx": ctx["vA"],
                                      "OH": X["OH"], "rel": X["rel"], "ctx01": c01, "mixT": mixT}, uid[0], last_only=lastA)
                kb.phase()
                phase_gla(kb, T, {"glT": own["glT"], "glT_ctx": ctx["glT"], "w2": X["c_gate_w2"][l], "cgb": X["c_gate_b"][l],
                                  "qCT": own["qCT"], "kCT": own["kCT"], "kC": own["kC"], "kC_ctx": ctx["kC"], "vC": own["vC"],
                                  "vC_ctx": ctx["vC"], "rC": own["rC"], "cn": X["c_norm"][l], "ctx01": c01, "mixT": mixT}, noctx=isA, last_only=lastA)
                kb.phase()
                phase_p2b(kb, T, {"mixT": mixT, "xT": xcur[ps_], "w_out": W[l]["w_out"], "g2": X["norm_cross"][l], "w_cq": W[l]["w_cq"],
                                  "w_ckv": W[l]["w_ckv"], "w_co": W[l]["w_co"], "memT": X["memT"], "gM": X["mem_norm"], "x2T": x2[ps_]},
                          last_only=lastA)
                final = l == depth - 1
                if ps_ == "A" and final:
                    continue
                kb.phase()
                phase_p3(kb, T, {"x2T": x2[ps_], "halo": x2["A"][:, T - 2:T], "ctx01": c01, "g3": X["norm_ffn"][l], "gF": X["norm_final"],
                                 "w_up": W[l]["w_up"], "conv_w": X["conv_w"][l], "conv_b": X["conv_b"][l], "w_down": W[l]["w_down"],
                                 "yT": yT if (final and ps_ == "B") else xn[ps_]}, final and ps_ == "B")
            xcur = {"A": xn["A"], "B": xn["B"]}
        kb.finalize()
        print("fused program: %d instructions, %d semaphores" % (kb.n, len(kb.sems)), flush=True)
    return nc


def kernel(x, mem, rel_table, mem_norm, norm_final, norm_mix, w_in, f_bias, c_gate_w2, c_gate_b, c_norm, w_out,
           norm_cross, w_cq, w_ckv, w_co, norm_ffn, w_up, conv_w, conv_b, w_down):
    import ml_dtypes
    f = lambda a: np.ascontiguousarray(np.asarray(a, dtype=np.float32))
    x, mem = f(x), f(mem)
    Bn, S, _ = x.shape
    T = S // 2
    depth = np.asarray(w_in).shape[0]
    cores = [(b, h) for b in range(Bn) for h in range(2)]
    maskB = np.zeros((4, 128, 512), np.float32)
    for kk in range(4):
        maskB[kk] = (128 * kk + np.arange(128)[:, None] <= np.arange(512)[None, :])
    maskB = maskB.astype(ml_dtypes.bfloat16)
    shared = {"maskB": maskB, "OH": dil_onehot(), "rel": f(rel_table), "norm_mix": f(norm_mix), "w_in": f(w_in), "f_bias": f(f_bias),
              "c_gate_w2": f(c_gate_w2), "c_gate_b": f(c_gate_b), "c_norm": f(c_norm), "w_out": f(w_out), "norm_cross": f(norm_cross),
              "w_cq": f(w_cq), "w_ckv": f(w_ckv), "w_co": f(w_co), "norm_ffn": f(norm_ffn), "w_up": f(w_up), "conv_w": f(conv_w),
              "conv_b": f(conv_b), "w_down": f(w_down), "mem_norm": f(mem_norm), "norm_final": f(norm_final),
              "ctx01A": np.zeros((128, 1), np.float32), "ctxnegA": np.full((128, 1), NEG, np.float32)}
    memT = [np.ascontiguousarray(mem[b].T) for b in range(Bn)]
    xTh = {(b, h): np.ascontiguousarray(x[b, h * T:(h + 1) * T].T) for b, h in cores}
    in_maps = []
    for b, h in cores:
        m = dict(shared)
        m.update({"xTA": xTh[(b, 0)], "xTB": xTh[(b, h)], "memT": memT[b],
                  "ctx01B": np.full((128, 1), float(h), np.float32),
                  "ctxnegB": np.full((128, 1), 0.0 if h == 1 else NEG, np.float32)})
        in_maps.append(m)
    nc = build_fused(T, depth)
    res = run_bass_kernel_spmd(nc, in_maps, core_ids=list(range(len(cores))))
    out = np.empty((Bn, S, D), np.float32)
    for c, (b, h) in enumerate(cores):
        out[b, h * T:(h + 1) * T] = np.asarray(res.results[c]["yT"]).T
    return out
```

```python
import contextlib
import math
import numpy as np
import concourse.bass as bass
import concourse.mybir as mybir
from concourse.bass_utils import run_bass_kernel_spmd

F32 = mybir.dt.float32
BF16 = mybir.dt.bfloat16
ALU = mybir.AluOpType
AF = mybir.ActivationFunctionType
AX = mybir.AxisListType

D = 2048
KC = D // 128
HD = 64
A_W, A_H = 768, 12
B_W, B_H = 512, 8
C_H, C_DV, C_DK, C_KW, C_VW = 4, 192, 96, 384, 768
C_RANK = 16
PROJ_W = 6168
O_QA, O_KA, O_VA, O_QB, O_KB, O_VB, O_FL, O_QC, O_KC, O_VC, O_RC, O_GL = (
    0, 768, 1536, 2304, 2816, 3328, 3840, 3848, 4232, 4616, 5384, 6152)
MEM = 256
X_H, X_DH, X_W = 4, 128, 512
DFF = 5504
FC = DFF // 128
EPS = 1e-6
NEG = -30000.0
SELF_SYNC = True


class Sem:
    __slots__ = ("name", "h", "count", "is_dma")

    def __init__(self, name, h, is_dma):
        self.name, self.h, self.count, self.is_dma = name, h, 0, is_dma


class Buf:
    __slots__ = ("name", "w", "r")

    def __init__(self, name=""):
        self.name, self.w, self.r = name, {}, {}


class Instr:
    __slots__ = ("eng", "fn", "deps", "signal", "sem", "val", "idx")

    def __init__(self, eng, fn, sem):
        self.eng, self.fn, self.sem = eng, fn, sem
        self.deps, self.signal, self.val = {}, False, None


class KB:
    ENGS = ("pe", "act", "dve", "pool", "sp")
    BLK = {"pe": "tensor", "act": "scalar", "dve": "vector", "pool": "gpsimd", "sp": "sync"}

    def __init__(self, nc, stack, arena_words=48 * 1024):
        self.nc, self.stack = nc, stack
        self.prog = {e: [] for e in self.ENGS}
        self.esem = {e: self.new_sem("e_" + e, False) for e in self.ENGS}
        self.n = 0
        self.last = {}
        self.pend = {e: {} for e in self.ENGS}
        self.arena = stack.enter_context(nc.sbuf_tensor("arena", [128, arena_words], F32))
        self.arena_words = arena_words
        self.top = 0
        self.psum = stack.enter_context(nc.psum_tensor("psum", [128, 8, 512], F32))
        self.pbank = [Buf("bank%d" % i) for i in range(8)]
        self.dq = 0

    def new_sem(self, name, is_dma=True):
        if not hasattr(self, "sems"):
            self.sems = []
            self.free_sems = []
        if is_dma and self.free_sems:
            return self.free_sems.pop()
        sm = Sem(name, self.stack.enter_context(self.nc.semaphore(name)), is_dma)
        self.sems.append(sm)
        return sm

    def alloc(self, shape, dt):
        p = shape[0]
        free = int(np.prod(shape[1:]))
        words = free if dt == F32 else (free + 1) // 2
        words = (words + 7) // 8 * 8
        assert self.top + words <= self.arena_words, ("SBUF arena overflow", self.top, words)
        v = self.arena[0:p, self.top:self.top + words]
        self.top += words
        if dt != F32:
            v = v.bitcast(dt)[:, 0:free]
        else:
            v = v[:, 0:free]
        if len(shape) == 3:
            v = v.rearrange("p (a b) -> p a b", a=shape[1])
        elif len(shape) == 4:
            v = v.rearrange("p (a b c) -> p a b c", a=shape[1], b=shape[2])
        return v

    def phase(self):
        self.top = 0
        self.free_sems = [sm for sm in self.sems if sm.is_dma]
        snap = dict(self.last)
        for e in self.ENGS:
            for s, i in snap.items():
                o = self.pend[e].get(s)
                if o is None or o.idx < i.idx:
                    self.pend[e][s] = i
        for i in snap.values():
            i.signal = True

    def _rec(self, eng, fn, sem, reads, writes, skip_self, touch=()):
        ins = Instr(eng, fn, sem)
        ins.idx = self.n
        self.n += 1
        deps = ins.deps

        def add(ev):
            for s, i in ev.items():
                if skip_self and s is sem:
                    continue
                o = deps.get(s)
                if o is None or o.idx < i.idx:
                    deps[s] = i

        if self.pend[eng]:
            add(self.pend[eng])
            self.pend[eng] = {}
        for b in reads:
            add(b.w)
        for b in writes:
            add(b.w)
            add(b.r)
        for i in deps.values():
            i.signal = True
        for b in reads:
            b.r[sem] = ins
        for b in writes:
            b.w[sem] = ins
        for b in touch:
            b.w[sem] = ins
        self.prog[eng].append(ins)
        self.last[sem] = ins
        return ins

    def op(self, eng, fn, reads=(), writes=()):
        skip = (eng == "pe") or (not SELF_SYNC)
        return self._rec(eng, fn, self.esem[eng], reads, writes, skip)

    def mm(self, out, lhsT, rhs, start, stop, reads, writes, **kw):
        return self.op("pe", lambda e: e.matmul(out, lhsT=lhsT, rhs=rhs, start=start, stop=stop, **kw), reads, writes)

    def tr(self, out, in_, ident, reads, writes):
        return self.op("pe", lambda e: e.transpose(out, in_, ident), reads, writes)

    def copy(self, eng, out, in_, reads, writes):
        if eng == "act":
            return self.op("act", lambda e: e.copy(out=out, in_=in_), reads, writes)
        return self.op(eng, lambda e: e.tensor_copy(out=out, in_=in_), reads, writes)

    def act(self, out, in_, func, reads, writes, **kw):
        return self.op("act", lambda e: e.activation(out=out, in_=in_, func=func, **kw), reads, writes)

    def tt(self, eng, out, in0, in1, op, reads, writes):
        return self.op(eng, lambda e: e.tensor_tensor(out=out, in0=in0, in1=in1, op=op), reads, writes)

    def ts(self, eng, out, in0, s1, s2, op0, op1, reads, writes):
        if op1 is None:
            return self.op(eng, lambda e: e.tensor_scalar(out=out, in0=in0, scalar1=s1, scalar2=None, op0=op0), reads, writes)
        return self.op(eng, lambda e: e.tensor_scalar(out=out, in0=in0, scalar1=s1, scalar2=s2, op0=op0, op1=op1), reads, writes)

    def stt(self, eng, out, in0, scalar, in1, op0, op1, reads, writes):
        return self.op(eng, lambda e: e.scalar_tensor_tensor(out=out, in0=in0, scalar=scalar, in1=in1, op0=op0, op1=op1), reads, writes)

    def recip(self, out, in_, reads, writes):
        return self.op("dve", lambda e: e.reciprocal(out=out, in_=in_), reads, writes)

    def memset(self, eng, out, val, writes):
        return self.op(eng, lambda e: e.memset(out, val), (), writes)

    def dma(self, out, in_, sem, reads=(), writes=(), q=None, touch=(), **kw):
        if q is None:
            q = ("sp", "pool")[self.dq & 1]
            self.dq += 1

        def fn(e, out=out, in_=in_, kw=kw):
            return e.dma_start(out=out, in_=in_, **kw)
        ins = self._rec(q, fn, sem, reads, writes, True, touch)
        ins.signal = True
        return ins

    def finalize(self):
        final_sems = [sm for sm in self.sems if sm.is_dma]
        allins = sorted((i for e in self.ENGS for i in self.prog[e]), key=lambda i: i.idx)
        for i in allins:
            if i.signal:
                i.sem.count += 16 if i.sem.is_dma else 1
                i.val = i.sem.count
        with self.nc.Block() as block:
            for e in self.ENGS:
                prog = self.prog[e]
                fin = final_sems if e == "sp" else ()

                def body(eng, prog=prog, fin=fin):
                    known = {}
                    for ins in prog:
                        for s, d in ins.deps.items():
                            if known.get(s, 0) < d.val:
                                eng.wait_ge(s.h, d.val)
                                known[s] = d.val
                        r = ins.fn(eng)
                        if ins.signal:
                            r.then_inc(ins.sem.h, 16 if ins.sem.is_dma else 1)
                    for s in fin:
                        if s.count > 0:
                            eng.wait_ge(s.h, s.count)

                getattr(block, self.BLK[e])(body)
        return self.nc


class Ring:
    def __init__(self, kb, name, n, shape, dt, dma=True):
        self.t = [kb.alloc(shape, dt) for _ in range(n)]
        self.b = [Buf("%s%d" % (name, i)) for i in range(n)]
        self.s = [kb.new_sem("%s%d" % (name, i)) for i in range(n)] if dma else [None] * n
        self.i = -1
        self.n = n

    def next(self):
        self.i = (self.i + 1) % self.n
        return self.t[self.i], self.b[self.i], self.s[self.i]


class PsRing:
    def __init__(self, kb, banks):
        self.kb, self.banks, self.i = kb, banks, -1

    def next(self):
        self.i = (self.i + 1) % len(self.banks)
        k = self.banks[self.i]
        return self.kb.psum[:, k, :], self.kb.pbank[k]


def load_cast(kb, stage, dst, dstbuf, src_ap, shape_sl, cast_eng):
    st, sb, ss = stage.next()
    sv = st[shape_sl]
    kb.dma(sv, src_ap, ss, writes=[sb])
    kb.copy(cast_eng, dst, sv, [sb], [dstbuf])


class NormPass:
    def __init__(self, kb, consts, TBN=256, banks=(0, 1)):
        self.kb, self.consts, self.TBN = kb, consts, TBN
        self.xs = Ring(kb, "xs", 2, [128, KC, TBN], F32)
        self.sq = Ring(kb, "sq", 2, [128, TBN], BF16, dma=False)
        self.rs = Ring(kb, "rs", 2, [128, TBN], F32, dma=False)
        self.pr = PsRing(kb, list(banks))

    def run(self, xT, gvec, gb, T0, NT, hT, hbuf):
        kb, consts, TBN = self.kb, self.consts, self.TBN
        ones_bf, cbuf = consts["ones_bf"], consts["buf"]
        for b in range(NT // TBN):
            t0 = T0 + b * TBN
            xt, xb, xsem = self.xs.next()
            kb.dma(xt, xT[:, t0:t0 + TBN].rearrange("(k p) t -> p k t", p=128), xsem, writes=[xb])
            ps, pb = self.pr.next()
            psv = ps[:, 0:TBN]
            for k in range(KC):
                sqt, sqb, _ = self.sq.next()
                kb.tt("pool" if k % 2 == 0 else "dve", sqt, xt[:, k, :], xt[:, k, :], ALU.mult, [xb], [sqb])
                kb.mm(psv, ones_bf, sqt, k == 0, k == KC - 1, [sqb, cbuf], [pb])
            rt, rb, _ = self.rs.next()
            kb.act(rt, psv, AF.Sqrt, [pb, cbuf], [rb], scale=1.0 / D, bias=consts["eps"])
            kb.recip(rt, rt, [rb], [rb])
            for k in range(KC):
                kb.stt("dve", hT[:, k, b * TBN:(b + 1) * TBN], xt[:, k, :],
                       gvec[:, k:k + 1], rt, ALU.mult, ALU.mult, [xb, rb, gb], [hbuf])


def make_consts(kb):
    c = {}
    cb = Buf("consts")
    c["ones_bf"] = kb.alloc([128, 128], BF16)
    kb.memset("dve", c["ones_bf"], 1.0, [cb])
    c["eps"] = kb.alloc([128, 1], F32)
    kb.memset("dve", c["eps"], EPS, [cb])
    c["buf"] = cb
    return c


FM_GROUPS = (
    ("qAT", O_QA, 128, 6), ("kAT", O_KA, 128, 6), ("qBT", O_QB, 128, 4), ("kBT", O_KB, 128, 4),
    ("qCT", O_QC, 96, 4), ("kCT", O_KC, 96, 4), ("glT", O_GL, 16, 1))
TM_GROUPS = (
    ("vA", O_VA, 768, 384), ("vB", O_VB, 512, 512), ("fl", O_FL, 8, 8), ("kC", O_KC, 384, 384),
    ("vC", O_VC, 768, 384), ("rC", O_RC, 768, 384))


def phase_p1(kb, T, io, NS=2048):
    NS = min(NS, T)
    consts = make_consts(kb)
    gvec = kb.alloc([128, KC], F32)
    gb = Buf("g")
    gsem = kb.new_sem("gsem")
    kb.dma(gvec, io["g1"].rearrange("(k p) -> p k", p=128), gsem, writes=[gb], allow_slow_non_contiguous=True)
    hT = kb.alloc([128, KC, NS], BF16)
    hbuf = Buf("hT")
    normp = NormPass(kb, consts)
    wbf = Ring(kb, "wbf", 3, [128, KC, 512], BF16)
    ofm = Ring(kb, "ofm", 2, [128, NS], BF16)
    ofm32 = Ring(kb, "ofm32", 1, [16, NS], F32)
    otm = Ring(kb, "otm", 3, [128, 512], BF16)
    otm32 = Ring(kb, "otm32", 2, [128, 8], F32)
    pr = PsRing(kb, [2, 3, 4, 5, 6, 7])
    obuf = Buf("p1out")
    ci = 0
    for sb in range(T // NS):
        T0 = sb * NS
        normp.run(io["xT"], gvec, gb, T0, NS, hT, hbuf)
        for name, c0, rows, nblk in FM_GROUPS:
            dst = io[name]
            for j in range(nblk):
                wt, wb, wsem_ = wbf.next()
                wv = wt[:, :, 0:rows]
                kb.dma(wv, io["w_in"][:, c0 + j * rows:c0 + (j + 1) * rows].rearrange("(k p) m -> p k m", p=128), wsem_, writes=[wb])
                ot, ob, os_ = (ofm32 if name == "glT" else ofm).next()
                for tb in range(NS // 512):
                    ps, pb = pr.next()
                    for k in range(KC):
                        kb.mm(ps[0:rows, :], wv[:, k, :], hT[:, k, tb * 512:(tb + 1) * 512], k == 0, k == KC - 1, [wb, hbuf], [pb])
                    kb.copy(("dve", "act")[tb & 1], ot[0:rows, tb * 512:(tb + 1) * 512], ps[0:rows, :], [pb], [ob])
                kb.dma(dst[j * rows:(j + 1) * rows, T0:T0 + NS], ot[0:rows, :], os_, reads=[ob], touch=[obuf])
        for name, c0, tot, cw in TM_GROUPS:
            dst = io[name]
            for j in range(tot // cw):
                wt, wb, wsem_ = wbf.next()
                wv = wt[:, :, 0:cw]
                kb.dma(wv, io["w_in"][:, c0 + j * cw:c0 + (j + 1) * cw].rearrange("(k p) m -> p k m", p=128), wsem_, writes=[wb])
                for tt in range(NS // 128):
                    ps, pb = pr.next()
                    for k in range(KC):
                        kb.mm(ps[:, 0:cw], hT[:, k, tt * 128:(tt + 1) * 128], wv[:, k, :], k == 0, k == KC - 1, [wb, hbuf], [pb])
                    ot, ob, os_ = (otm32 if name == "fl" else otm).next()
                    kb.copy(("dve", "act")[tt & 1], ot[:, 0:cw], ps[:, 0:cw], [pb], [ob])
                    kb.dma(dst[T0 + tt * 128:T0 + (tt + 1) * 128, j * cw:(j + 1) * cw], ot[:, 0:cw], os_, reads=[ob], touch=[obuf])
    return obuf


class NormSB:
    def __init__(self, kb, consts, n, banks=(0, 1)):
        self.kb, self.consts, self.n = kb, consts, n
        self.sq = Ring(kb, "nsq", 2, [128, n], BF16, dma=False)
        self.rs = Ring(kb, "nrs", 1, [128, n], F32, dma=False)
        self.pr = PsRing(kb, list(banks))

    def rstd(self, xt, xb, c0, n):
        kb, consts = self.kb, self.consts
        ps, pb = self.pr.next()
        psv = ps[:, 0:n]
        for k in range(KC):
            sqt, sqb, _ = self.sq.next()
            kb.tt("pool" if k % 2 == 0 else "dve", sqt[:, 0:n], xt[:, k, c0:c0 + n], xt[:, k, c0:c0 + n], ALU.mult, [xb], [sqb])
            kb.mm(psv, consts["ones_bf"], sqt[:, 0:n], k == 0, k == KC - 1, [sqb, consts["buf"]], [pb])
        rt, rb, _ = self.rs.next()
        rv = rt[:, 0:n]
        kb.act(rv, psv, AF.Sqrt, [pb, consts["buf"]], [rb], scale=1.0 / D, bias=consts["eps"])
        kb.recip(rv, rv, [rb], [rb])
        return rv, rb

    def apply(self, xt, xb, c0, n, gvec, gb, out, obuf, oc0):
        rv, rb = self.rstd(xt, xb, c0, n)
        for k in range(KC):
            self.kb.stt("dve", out[:, k, oc0:oc0 + n], xt[:, k, c0:c0 + n], gvec[:, k:k + 1], rv, ALU.mult, ALU.mult,
                        [xb, rb, gb], [obuf])


def load_vec(kb, dram_vec, nchunk, name):
    t = kb.alloc([128, nchunk], F32)
    b = Buf(name)
    s = kb.new_sem(name)
    kb.dma(t, dram_vec.rearrange("(k p) -> p k", p=128), s, writes=[b], allow_slow_non_contiguous=True)
    return t, b


def load_weight_resident(kb, wst, dst, dbuf, w_ap, ncol, cw, ci0=0):
    ci = ci0
    for c in range(0, ncol, cw):
        load_cast(kb, wst, dst[:, :, c:c + cw], dbuf, w_ap[:, c:c + cw].rearrange("(k p) m -> p k m", p=128),
                  (slice(None), slice(0, dst.shape[1]), slice(0, cw)), ("act", "pool")[ci & 1])
        ci += 1


def phase_cast(kb, pairs, CW=2048):
    st = Ring(kb, "cst", 3, [128, CW], F32)
    bt = Ring(kb, "cbt", 3, [128, CW], BF16)
    ob = Buf("castout")
    ci = 0
    for src, dst in pairs:
        R, C = src.shape
        for n in range(R // 128):
            for c0 in range(0, C, CW):
                cw = min(CW, C - c0)
                s_, sb, ss = st.next()
                kb.dma(s_[:, 0:cw], src[n * 128:(n + 1) * 128, c0:c0 + cw], ss, writes=[sb])
                b_, bb, bs = bt.next()
                kb.copy(("act", "pool", "dve")[ci % 3], b_[:, 0:cw], s_[:, 0:cw], [sb], [bb])
                ci += 1
                kb.dma(dst[n * 128:(n + 1) * 128, c0:c0 + cw], b_[:, 0:cw], bs, reads=[bb], touch=[ob])


def phase_p3(kb, T, io, final):
    TB = 512 if T >= 512 else T
    consts = make_consts(kb)
    g3, g3b = load_vec(kb, io["g3"], KC, "g3")
    if final:
        gF, gFb = load_vec(kb, io["gF"], KC, "gF")
    cw = kb.alloc([128, 3, FC], F32)
    cwb = Buf("cw")
    cws = kb.new_sem("cw")
    for i in range(3):
        kb.dma(cw[:, i, :], io["conv_w"][i, :].rearrange("(k p) -> p k", p=128), cws, writes=[cwb], allow_slow_non_contiguous=True)
    cbv, cbb = load_vec(kb, io["conv_b"], FC, "cb")
    ctx01 = kb.alloc([128, 1], F32)
    kb.dma(ctx01, io["ctx01"], cws, writes=[cwb])
    xs = Ring(kb, "xs", 1, [128, KC, TB + 2], F32)
    hT = kb.alloc([128, KC, TB + 2], BF16)
    hbuf = Buf("hT")
    aT = kb.alloc([128, FC, TB], BF16)
    abuf = Buf("aT")
    nrm = NormSB(kb, consts, TB, banks=(0,))
    wbf = Ring(kb, "wbf", 3, [128, KC, 512], BF16)
    gts = Ring(kb, "gt", 2, [128, TB + 2], F32, dma=False)
    t1s = Ring(kb, "t1", 2, [128, TB], F32, dma=False)
    t2s = Ring(kb, "t2", 2, [128, TB], F32, dma=False)
    wds = Ring(kb, "wd", 3, [128, 4, 512], BF16)
    outr = Ring(kb, "outr", 2, [128, TB], F32)
    pr = PsRing(kb, [1, 2, 3, 4, 5, 6])
    phalo = kb.psum[:, 7, :]
    phb = kb.pbank[7]
    obuf = Buf("p3out")
    NG = (FC + 3) // 4
    for b in range(T // TB):
        t0 = b * TB
        xt, xb, xsem = xs.next()
        if b == 0:
            kb.dma(xt[:, :, 0:2], io["halo"].rearrange("(k p) t -> p k t", p=128), xsem, writes=[xb], allow_slow_non_contiguous=True)
            kb.dma(xt[:, :, 2:TB + 2], io["x2T"][:, 0:TB].rearrange("(k p) t -> p k t", p=128), xsem, writes=[xb])
            kb.ts("pool", xt[:, :, 0:2], xt[:, :, 0:2], ctx01[:, 0:1], None, ALU.mult, None, [xb, cwb], [xb])
        else:
            kb.dma(xt, io["x2T"][:, t0 - 2:t0 + TB].rearrange("(k p) t -> p k t", p=128), xsem, writes=[xb])
        nrm.apply(xt, xb, 2, TB, g3, g3b, hT, hbuf, 2)
        nrm.apply(xt, xb, 0, 2, g3, g3b, hT, hbuf, 0)
        for g in range(NG):
            c0 = g * 512
            ncol = min(512, DFF - c0)
            wv, wvb, wvs = wbf.next()
            kb.dma(wv[:, :, 0:ncol], io["w_up"][:, c0:c0 + ncol].rearrange("(k p) m -> p k m", p=128), wvs, writes=[wvb])
            wg, wgb, wgs = wbf.next()
            kb.dma(wg[:, :, 0:ncol], io["w_up"][:, DFF + c0:DFF + c0 + ncol].rearrange("(k p) m -> p k m", p=128), wgs, writes=[wgb])
            for jj in range(ncol // 128):
                j = 4 * g + jj
                cs = slice(jj * 128, (jj + 1) * 128)
                pv, pvb = pr.next()
                for k in range(KC):
                    kb.mm(pv[:, 0:TB], wv[:, k, cs], hT[:, k, 2:TB + 2], k == 0, k == KC - 1, [wvb, hbuf], [pvb])
                pg, pgb = pr.next()
                for k in range(KC):
                    kb.mm(pg[:, 0:TB], wg[:, k, cs], hT[:, k, 2:TB + 2], k == 0, k == KC - 1, [wgb, hbuf], [pgb])
                for k in range(KC):
                    kb.mm(phalo[:, 0:2], wg[:, k, cs], hT[:, k, 0:2], k == 0, k == KC - 1, [wgb, hbuf], [phb])
                gt, gtb, _ = gts.next()
                kb.copy("act", gt[:, 2:TB + 2], pg[:, 0:TB], [pgb], [gtb])
                kb.copy("dve", gt[:, 0:2], phalo[:, 0:2], [phb], [gtb])
                t1, t1b, _ = t1s.next()
                kb.ts("dve", t1, gt[:, 0:TB], cw[:, 0, j:j + 1], cbv[:, j:j + 1], ALU.mult, ALU.add, [gtb, cwb, cbb], [t1b])
                kb.stt("dve", t1, gt[:, 1:TB + 1], cw[:, 1, j:j + 1], t1, ALU.mult, ALU.add, [gtb, cwb, t1b], [t1b])
                kb.stt("dve", t1, gt[:, 2:TB + 2], cw[:, 2, j:j + 1], t1, ALU.mult, ALU.add, [gtb, cwb, t1b], [t1b])
                t2, t2b, _ = t2s.next()
                kb.act(t2, t1, AF.Silu, [t1b], [t2b])
                kb.tt("dve", aT[:, j, :], t2, pv[:, 0:TB], ALU.mult, [t2b, pvb], [abuf])
        for ig in range(KC // 4):
            banks = [1, 2, 3, 4]
            for g in range(NG):
                nj = min(4, FC - 4 * g)
                wd, wdb, wdsem = wds.next()
                kb.dma(wd[:, 0:nj, :], io["w_down"][g * 512:g * 512 + nj * 128, ig * 512:(ig + 1) * 512].rearrange("(j p) m -> p j m", p=128),
                       wdsem, writes=[wdb])
                for jj in range(nj):
                    j = 4 * g + jj
                    for i4 in range(4):
                        kb.mm(kb.psum[:, banks[i4], 0:TB], wd[:, jj, i4 * 128:(i4 + 1) * 128], aT[:, j, :], j == 0, j == FC - 1,
                              [wdb, abuf], [kb.pbank[banks[i4]]])
            for i4 in range(4):
                i = ig * 4 + i4
                po, pob = kb.psum[:, banks[i4], :], kb.pbank[banks[i4]]
                if not final:
                    ot, ob, os_ = outr.next()
                    kb.tt("dve", ot, po[:, 0:TB], xt[:, i, 2:TB + 2], ALU.add, [pob, xb], [ob])
                    kb.dma(io["yT"][i * 128:(i + 1) * 128, t0:t0 + TB], ot, os_, reads=[ob], touch=[obuf])
                else:
                    kb.tt("dve", xt[:, i, 2:TB + 2], po[:, 0:TB], xt[:, i, 2:TB + 2], ALU.add, [pob, xb], [xb])
        if final:
            rv, rb = nrm.rstd(xt, xb, 2, TB)
            for i in range(KC):
                ot, ob, os_ = outr.next()
                kb.stt("dve", ot, xt[:, i, 2:TB + 2], gF[:, i:i + 1], rv, ALU.mult, ALU.mult, [xb, rb, gFb], [ob])
                kb.dma(io["yT"][i * 128:(i + 1) * 128, t0:t0 + TB], ot, os_, reads=[ob], touch=[obuf])
    return obuf


def phase_p2b(kb, T, io, last_only=False):
    TB = 256
    consts = make_consts(kb)
    g2, g2b = load_vec(kb, io["g2"], KC, "g2")
    gM, gMb = load_vec(kb, io["gM"], KC, "gM")
    wsem = kb.new_sem("p2bw")
    wout = kb.alloc([128, KC, D], BF16); woutb = Buf("wout")
    wcq = kb.alloc([128, KC, X_W], BF16); wcqb = Buf("wcq")
    wco = kb.alloc([128, X_H, D], BF16); wcob = Buf("wco")
    kxT = kb.alloc([128, X_H, MEM], BF16); kxb = Buf("kxT")
    vx = kb.alloc([128, 2, X_W], BF16); vxb = Buf("vx")
    pr = PsRing(kb, [1, 2, 3, 4, 5, 6, 7])
    nrm = NormSB(kb, consts, TB, banks=(0,))
    xs = Ring(kb, "xs", 1, [128, KC, TB], F32)
    mts = Ring(kb, "mt", 2, [128, KC, TB], BF16)
    h2T = kb.alloc([128, KC, TB], BF16); h2b = Buf("h2T")
    q2T = kb.alloc([128, X_H, TB], BF16); q2b = Buf("q2T")
    PTs = Ring(kb, "PT", 4, [128, 2, TB], BF16, dma=False)
    rds = Ring(kb, "rd", 2, [128, TB], F32, dma=False)
    o2T = kb.alloc([128, X_H, TB], BF16); o2b = Buf("o2T")
    outs = kb.new_sem("x2out"); obuf = Buf("x2out")
    xt, xb, xsem = xs.next()
    kb.dma(xt[:, :, 0:MEM], io["memT"].rearrange("(k p) t -> p k t", p=128), xsem, writes=[xb])
    nrm.apply(xt, xb, 0, MEM, gM, gMb, h2T, h2b, 0)
    wkv = wout[:, :, 0:1024]
    for c in range(0, 1024, 512):
        kb.dma(wkv[:, :, c:c + 512], io["w_ckv"][:, c:c + 512].rearrange("(k p) m -> p k m", p=128), wsem, writes=[woutb])
    for h in range(X_H):
        ps, pb = pr.next()
        for k in range(KC):
            kb.mm(ps[:, 0:MEM], wkv[:, k, h * 128:(h + 1) * 128], h2T[:, k, 0:MEM], k == 0, k == KC - 1, [woutb, h2b], [pb])
        kb.copy(("dve", "act")[h & 1], kxT[:, h, :], ps[:, 0:MEM], [pb], [kxb])
    for mt in range(2):
        ps, pb = pr.next()
        for k in range(KC):
            kb.mm(ps[:, 0:X_W], h2T[:, k, mt * 128:(mt + 1) * 128], wkv[:, k, 512:1024], k == 0, k == KC - 1, [woutb, h2b], [pb])
        kb.copy(("dve", "act")[mt & 1], vx[:, mt, :], ps[:, 0:X_W], [pb], [vxb])
    for c in range(0, D, 512):
        kb.dma(wout[:, :, c:c + 512], io["w_out"][:, c:c + 512].rearrange("(k p) m -> p k m", p=128), wsem, writes=[woutb])
    kb.dma(wcq, io["w_cq"].rearrange("(k p) m -> p k m", p=128), wsem, writes=[wcqb])
    for c in range(0, D, 512):
        kb.dma(wco[:, :, c:c + 512], io["w_co"][:, c:c + 512].rearrange("(k p) m -> p k m", p=128), wsem, writes=[wcob])
    scale = float(X_DH) ** -0.5
    for b in range(T // TB):
        if last_only and b != T // TB - 1:
            continue
        t0 = b * TB
        xt, xb, xsem = xs.next()
        kb.dma(xt, io["xT"][:, t0:t0 + TB].rearrange("(k p) t -> p k t", p=128), xsem, writes=[xb])
        mt_, mb, msem = mts.next()
        kb.dma(mt_, io["mixT"][:, t0:t0 + TB].rearrange("(k p) t -> p k t", p=128), msem, writes=[mb])
        for i in range(KC):
            ps, pb = pr.next()
            for k in range(KC):
                kb.mm(ps[:, 0:TB], wout[:, k, i * 128:(i + 1) * 128], mt_[:, k, :], k == 0, k == KC - 1, [woutb, mb], [pb])
            kb.tt("dve", xt[:, i, :], ps[:, 0:TB], xt[:, i, :], ALU.add, [pb, xb], [xb])
        nrm.apply(xt, xb, 0, TB, g2, g2b, h2T, h2b, 0)
        for h in range(X_H):
            ps, pb = pr.next()
            for k in range(KC):
                kb.mm(ps[:, 0:TB], wcq[:, k, h * 128:(h + 1) * 128], h2T[:, k, :], k == 0, k == KC - 1, [wcqb, h2b], [pb])
            kb.copy(("dve", "act")[h & 1], q2T[:, h, :], ps[:, 0:TB], [pb], [q2b])
        PTl = []
        for h in range(X_H):
            PT, PTb, _ = PTs.next()
            ps, pb = pr.next()
            for mt in range(2):
                kb.mm(ps[:, mt * TB:(mt + 1) * TB], kxT[:, h, mt * 128:(mt + 1) * 128], q2T[:, h, :], True, True, [kxb, q2b], [pb])
            kb.act(PT.rearrange("p m t -> p (m t)"), ps[:, 0:2 * TB], AF.Exp, [pb], [PTb], scale=scale)
            PTl.append((PT, PTb))
        for h in range(X_H):
            PT, PTb = PTl[h]
            po, pob = pr.next()
            for mt in range(2):
                kb.mm(po[:, 0:TB], vx[:, mt, h * 128:(h + 1) * 128], PT[:, mt, :], mt == 0, mt == 1, [vxb, PTb], [pob])
            pd, pdb = pr.next()
            for mt in range(2):
                kb.mm(pd[:, 0:TB], consts["ones_bf"], PT[:, mt, :], mt == 0, mt == 1, [consts["buf"], PTb], [pdb])
            rd, rdb, _ = rds.next()
            kb.recip(rd, pd[:, 0:TB], [pdb], [rdb])
            kb.tt("dve", o2T[:, h, :], po[:, 0:TB], rd, ALU.mult, [pob, rdb], [o2b])
        for i in range(KC):
            ps, pb = pr.next()
            for h in range(X_H):
                kb.mm(ps[:, 0:TB], wco[:, h, i * 128:(i + 1) * 128], o2T[:, h, :], h == 0, h == X_H - 1, [wcob, o2b], [pb])
            kb.tt("dve", xt[:, i, :], ps[:, 0:TB], xt[:, i, :], ALU.add, [pb, xb], [xb])
        kb.dma(io["x2T"][:, t0:t0 + TB].rearrange("(k p) t -> p k t", p=128), xt, outs, reads=[xb], touch=[obuf])
    return obuf


def phase_fox(kb, T, io, noctx=False, last_only=False):
    NT = 2 * T // 128
    NCT = T // 128
    QB = 512
    consts = make_consts(kb)
    cb_ = consts["buf"]
    ones32 = kb.alloc([128, 128], F32)
    triU = kb.alloc([128, 128], F32)
    kb.memset("dve", ones32, 1.0, [cb_])
    kb.op("pool", lambda e: e.affine_select(out=triU, in_=ones32, pattern=[[1, 128]], compare_op=ALU.is_ge, fill=0.0,
                                            base=0, channel_multiplier=-1), [cb_], [cb_])
    mask = kb.alloc([128, 4, QB], BF16)
    msem = kb.new_sem("maskB")
    kb.dma(mask, io["maskB"].rearrange("k p c -> p k c"), msem, writes=[cb_])
    ctxneg = kb.alloc([128, 1], F32)
    kb.dma(ctxneg, io["ctxneg"], msem, writes=[cb_])
    fl = kb.alloc([128, NT, B_H], F32); flb = Buf("fl")
    fsem = kb.new_sem("flsem")
    kb.dma(fl[:, 0:NCT, :], io["fl_ctx"].rearrange("(n p) h -> p n h", p=128), fsem, writes=[flb])
    kb.dma(fl[:, NCT:NT, :], io["fl"].rearrange("(n p) h -> p n h", p=128), fsem, writes=[flb])
    fbt = kb.alloc([128, B_H], F32)
    kb.dma(fbt, io["fb"].partition_broadcast(128), fsem, writes=[flb])
    for n in range(NT):
        kb.tt("dve" if n % 2 else "pool", fl[:, n, :], fl[:, n, :], fbt, ALU.add, [flb], [flb])
    flf = fl.rearrange("p n h -> p (n h)")
    kb.act(flf, flf, AF.Exp, [flb], [flb], scale=-1.0)
    kb.act(flf, flf, AF.Ln, [flb], [flb], bias=1.0)
    cp = kb.alloc([128, NT, B_H], F32); cpb = Buf("cp")
    tot = kb.alloc([128, NT, B_H], F32)
    NF = NT * B_H
    for c0 in range(0, NF, 512):
        c1 = min(NF, c0 + 512)
        ps, pb = kb.psum[:, 0, :], kb.pbank[0]
        kb.mm(ps[:, 0:c1 - c0], triU, flf[:, c0:c1], True, True, [flb, cb_], [pb])
        kb.copy("dve", cp.rearrange("p n h -> p (n h)")[:, c0:c1], ps[:, 0:c1 - c0], [pb], [cpb])
        ps, pb = kb.psum[:, 1, :], kb.pbank[1]
        kb.mm(ps[:, 0:c1 - c0], ones32, flf[:, c0:c1], True, True, [flb, cb_], [pb])
        kb.copy("dve", tot.rearrange("p n h -> p (n h)")[:, c0:c1], ps[:, 0:c1 - c0], [pb], [cpb])
    carry = kb.alloc([128, NT + 1, B_H], F32)
    kb.memset("dve", carry[:, 0, :], 0.0, [cpb])
    for n in range(NT):
        kb.tt("dve", carry[:, n + 1, :], carry[:, n, :], tot[:, n, :], ALU.add, [cpb], [cpb])
        kb.tt("pool", cp[:, n, :], cp[:, n, :], carry[:, n, :], ALU.add, [cpb], [cpb])
    for n in range(NCT):
        kb.ts("pool", cp[:, n, :], cp[:, n, :], ctxneg[:, 0:1], None, ALU.add, None, [cpb, cb_], [cpb])
    qs = Ring(kb, "fq", 2, [128, 2 * T], BF16)
    for i_, t_ in enumerate(qs.t):
        kb.memset("pool", t_[0:64, T:2 * T], 0.0, [qs.b[i_]])
        kb.memset("pool", t_[64:128, 0:T], 0.0, [qs.b[i_]])
    ks = Ring(kb, "fk", 2, [128, 2 * T], BF16)
    vs = Ring(kb, "fv", 2, [128, NT, 128], BF16)
    for t_ in vs.t:
        kb.memset("pool", t_[:, :, 64:128], 1.0, [vs.b[vs.t.index(t_)]])
    biasr = Ring(kb, "fbias", 2, [128, T // QB, NT], F32, dma=False)
    PTs = Ring(kb, "fP", 4, [128, QB], BF16, dma=False)
    rds = Ring(kb, "frd", 2, [64, QB], F32, dma=False)
    outs = Ring(kb, "fout", 2, [64, T], BF16)
    prS = PsRing(kb, [2, 3, 4, 5])
    prO = PsRing(kb, [6, 7])
    obuf = Buf("foxout")
    LA = 2
    tiles = []
    heads = {}
    for h in range(B_H):
        for j in range(T // QB):
            if last_only and j != T // QB - 1:
                continue
            nk = NCT + 4 * j + 4
            n0 = NCT if noctx else 0
            for n in range(n0, nk):
                tiles.append((h, j, n, n == n0, n == nk - 1))
    state = {}

    pairs = {}

    def load_head(h):
        hp = h // 2
        if hp not in pairs:
            qt, qb, qsem = qs.next()
            kb.dma(qt[0:64, 0:T], io["qBT"][hp * 128:hp * 128 + 64, :], qsem, writes=[qb])
            kb.dma(qt[64:128, T:2 * T], io["qBT"][hp * 128 + 64:hp * 128 + 128, :], qsem, writes=[qb])
            kt, kbf, ksem = ks.next()
            kb.dma(kt[:, 0:T], io["kBT_ctx"][hp * 128:(hp + 1) * 128, :], ksem, writes=[kbf])
            kb.dma(kt[:, T:2 * T], io["kBT"][hp * 128:(hp + 1) * 128, :], ksem, writes=[kbf])
            pairs[hp] = (qt, qb, kt, kbf)
        qt, qb, kt, kbf = pairs[hp]
        vt, vb, vsem = vs.next()
        kb.dma(vt[:, 0:NCT, 0:64], io["vB_ctx"][:, h * 64:(h + 1) * 64].rearrange("(n p) e -> p n e", p=128), vsem, writes=[vb])
        kb.dma(vt[:, NCT:NT, 0:64], io["vB"][:, h * 64:(h + 1) * 64].rearrange("(n p) e -> p n e", p=128), vsem, writes=[vb])
        bias_, biasb_, _ = biasr.next()
        for j in range(T // QB):
            kb.ts("dve", bias_[:, j, :], cp[:, :, h], carry[:, NCT + 4 * j, h:h + 1], None, ALU.subtract, None, [cpb], [biasb_])
        ot, ob, osem = outs.next()
        heads[h] = (qt, qb, kt, kbf, vt, vb, bias_, biasb_, ot, ob, osem)

    def stage1(i):
        h, j, n, first, last = tiles[i]
        if h not in heads:
            load_head(h)
        qt, qb, kt, kbf, vt, vb, bias_, biasb_, ot, ob, osem = heads[h]
        ps, pb = prS.next()
        qo = (h % 2) * T + j * QB
        kb.mm(ps, kt[:, n * 128:(n + 1) * 128], qt[:, qo:qo + QB], True, True, [kbf, qb], [pb])
        PT, PTb, _ = PTs.next()
        kb.act(PT, ps, AF.Exp, [pb, biasb_], [PTb], scale=0.125, bias=bias_[:, j, n:n + 1])
        d = n - (NCT + 4 * j)
        if d >= 0:
            kb.tt("dve", PT, PT, mask[:, d, :], ALU.mult, [PTb, cb_], [PTb])
        state[i] = (PT, PTb)

    def stage2(i):
        h, j, n, first, last = tiles[i]
        qt, qb, kt, kbf, vt, vb, bias_, biasb_, ot, ob, osem = heads[h]
        PT, PTb = state.pop(i)
        if first:
            state["po"] = prO.next()
        po, pob = state["po"]
        kb.mm(po, vt[:, n, :], PT, first, last, [vb, PTb], [pob])
        if last:
            rd, rdb, _ = rds.next()
            kb.recip(rd, po[64:128, :], [pob], [rdb])
            kb.tt("dve", ot[:, j * QB:(j + 1) * QB], po[0:64, :], rd, ALU.mult, [pob, rdb], [ob])
            if j == T // QB - 1:
                kb.dma(io["mixT"][A_W + h * 64:A_W + (h + 1) * 64, :], ot, osem, reads=[ob], touch=[obuf])

    N = len(tiles)
    for i in range(N + LA):
        if i < N:
            stage1(i)
        if i >= LA:
            stage2(i - LA)
    return obuf


A_PAT = (1, 4, 16)


def phase_dil(kb, nc, T, io, uid, last_only=False):
    consts = make_consts(kb)
    cb_ = consts["buf"]
    ctx01 = kb.alloc([128, 1], F32)
    csem = kb.new_sem("dconst")
    kb.dma(ctx01, io["ctx01"], csem, writes=[cb_])
    rel = kb.alloc([32, A_H], F32)
    kb.dma(rel, io["rel"], csem, writes=[cb_])
    oh = kb.alloc([32, 3, 129], F32)
    kb.dma(oh, io["OH"].rearrange("q b m -> b q m"), csem, writes=[cb_])
    ones32 = kb.alloc([128, 128], F32)
    J = kb.alloc([128, 128], F32)
    kb.memset("dve", ones32, 1.0, [cb_])
    kb.op("pool", lambda e: e.affine_select(out=J, in_=ones32, pattern=[[-1, 128]], compare_op=ALU.is_equal, fill=0.0,
                                            base=127, channel_multiplier=-1), [cb_], [cb_])
    identb = kb.alloc([128, 128], BF16)
    kb.op("pool", lambda e: e.affine_select(out=identb, in_=consts["ones_bf"], pattern=[[-1, 128]], compare_op=ALU.is_equal,
                                            fill=0.0, base=0, channel_multiplier=1), [cb_], [cb_])
    ctxneg8 = kb.alloc([128, 1], F32)
    kb.dma(ctxneg8, io["ctxneg"], csem, writes=[cb_])
    kb.ts("dve", ctxneg8, ctxneg8, 8.0, None, ALU.mult, None, [cb_], [cb_])
    gsc_t = nc.dram_tensor("gsc%d" % uid, [3 * A_H * 2 * 256], F32)
    gsc = gsc_t.ap()
    gsem = kb.new_sem("gsc"); gscb = Buf("gsc")
    fe = kb.alloc([A_H, 129], F32); feb = Buf("fe")
    G = kb.alloc([A_H, 2, 256], F32); Gb = Buf("G")
    EB32 = kb.alloc([128, A_H, 256], F32); EB32b = Buf("EB32"); ebsem = kb.new_sem("eb32")
    EB = [kb.alloc([128, A_H, 256], BF16) for _ in range(3)]
    EB0 = [kb.alloc([128, A_H, 256], BF16) for _ in range(3)]
    EBb = Buf("EB")
    prE = PsRing(kb, [2, 3, 4, 5])
    for p in range(3):
        ps, pb = kb.psum[:, 0, :], kb.pbank[0]
        kb.mm(ps[0:A_H, 0:129], rel, oh[:, p, :], True, True, [cb_], [pb])
        kb.ts("dve", fe, ps[0:A_H, 0:129], 8.0, None, ALU.mult, None, [pb], [feb])
        kb.memset("dve", G.rearrange("h w m -> h (w m)"), 8.0 * NEG, [Gb])
        kb.copy("dve", G[:, 0, 0:128], fe[:, 1:129], [feb], [Gb])
        kb.copy("dve", G[:, 1, 127:256], fe[:, 0:129], [feb], [Gb])
        kb.dma(gsc[p * A_H * 512:(p + 1) * A_H * 512].rearrange("(h m) -> h m", h=A_H), G.rearrange("h w m -> h (w m)"),
               gsem, reads=[Gb], writes=[gscb])
        for h in range(A_H):
            for w in range(2):
                src = bass.AP(gsc_t, (p * A_H + h) * 512 + w * 256, [[1, 128], [1, 128]])
                kb.dma(EB32[:, h, w * 128:(w + 1) * 128], src, ebsem, reads=[gscb], writes=[EB32b])
        for h in range(A_H):
            ps, pb = prE.next()
            kb.mm(ps[:, 0:256], J, EB32[:, h, :], True, True, [cb_, EB32b], [pb])
            kb.copy("act", EB[p][:, h, :], ps[:, 0:256], [pb], [EBb])
            kb.copy("act", EB0[p][:, h, 128:256], ps[:, 128:256], [pb], [EBb])
            kb.ts("dve", EB0[p][:, h, 0:128], ps[:, 0:128], ctxneg8[:, 0:1], None, ALU.add, None, [pb, cb_], [EBb])
    CA = min(2048, T)
    KOFF = 2048
    qs = Ring(kb, "dq", 2, [128, 2 * T], BF16)
    for i_, t_ in enumerate(qs.t):
        kb.memset("pool", t_[0:64, T:2 * T], 0.0, [qs.b[i_]])
        kb.memset("pool", t_[64:128, 0:T], 0.0, [qs.b[i_]])
    ks = Ring(kb, "dk", 2, [128, KOFF + T], BF16)
    dpairs = {}
    NVT = (T + 2048) // 128
    vs = Ring(kb, "dv", 2, [128, NVT, 128], BF16)
    for i_, t_ in enumerate(vs.t):
        kb.memset("pool", t_[:, :, 64:128], 1.0, [vs.b[i_]])
    accs = Ring(kb, "dacc", 2, [128, T], F32, dma=False)
    PTs = Ring(kb, "dP", 4, [128, 256], BF16, dma=False)
    OC = min(1024, T)
    rds = Ring(kb, "drd", 2, [64, OC], F32, dma=False)
    outs = Ring(kb, "dout", 2, [64, OC], BF16)
    prS = PsRing(kb, [2, 3, 4, 5])
    prO = PsRing(kb, [6, 7, 0, 1])
    obuf = Buf("dilout")
    LA = 2
    tiles = []
    for h in range(A_H):
        for p, d in enumerate(A_PAT):
            NU = (T // d) // 128
            for r in range(d):
                for u in range(NU):
                    if last_only and u != NU - 1:
                        continue
                    tiles.append((h, p, d, r, u))
    last_of_head = {}
    for i, t_ in enumerate(tiles):
        last_of_head[t_[0]] = i
    hd, hv, state = {}, {}, {}

    def load_head(h):
        hr = slice(h * 64, (h + 1) * 64)
        acc, accb0, _ = accs.next()
        if not hasattr(accs, "fine"):
            accs.fine = {}
        accb = accs.fine.setdefault(id(accb0), [Buf("accf%d" % i_) for i_ in range(T // 128)])
        hp = h // 2
        if hp not in dpairs:
            pr_ = slice(hp * 128, (hp + 1) * 128)
            qt, qb, qsem = qs.next()
            kb.dma(qt[0:64, 0:T], io["qAT"][hp * 128:hp * 128 + 64, :], qsem, writes=[qb])
            kb.dma(qt[64:128, T:2 * T], io["qAT"][hp * 128 + 64:hp * 128 + 128, :], qsem, writes=[qb])
            kt, kbf, ksem = ks.next()
            kb.dma(kt[:, KOFF - CA:KOFF], io["kAT_ctx"][pr_, T - CA:T], ksem, writes=[kbf])
            kb.dma(kt[:, KOFF:KOFF + T], io["kAT"][pr_, :], ksem, writes=[kbf])
            dpairs[hp] = (qt, qb, kt, kbf)
        qt, qb, kt, kbf = dpairs[hp]
        hd[h] = (acc, accb, qt, qb, kt, kbf)

    def load_v(h, p, d):
        hr = slice(h * 64, (h + 1) * 64)
        NU = (T // d) // 128
        vt, vb, vsem = vs.next()
        for r in range(d):
            kb.dma(vt[:, r * (1 + NU), 0:64],
                   io["vA_ctx"][T - 128 * d:T, hr].rearrange("(p d) e -> d p e", d=d)[r], vsem, writes=[vb])
            kb.dma(vt[:, r * (1 + NU) + 1:(r + 1) * (1 + NU), 0:64],
                   io["vA"][:, hr].rearrange("(n p d) e -> d p n e", p=128, d=d)[r], vsem, writes=[vb])
        hv[(h, p)] = (vt, vb)

    def stage1(i):
        h, p, d, r, u = tiles[i]
        if h not in hd:
            load_head(h)
        if (h, p) not in hv:
            load_v(h, p, d)
        acc, accb, qt, qb, kt, kbf = hd[h]
        q0 = r + 128 * u * d
        qq = (h % 2) * T + q0
        qv = qt[:, qq:qq + 127 * d + 1:d]
        k0 = KOFF - 128 * d + r + 128 * u * d
        k1 = k0 + 128 * d
        ps, pb = prS.next()
        kb.mm(ps[:, 0:256], identb, (EB0 if u == 0 else EB)[p][:, h, :], True, False, [cb_, EBb], [pb])
        kb.mm(ps[:, 0:128], kt[:, k0:k0 + 127 * d + 1:d], qv, False, False, [kbf, qb], [pb])
        kb.mm(ps[:, 128:256], kt[:, k1:k1 + 127 * d + 1:d], qv, False, True, [kbf, qb], [pb])
        PT, PTb, _ = PTs.next()
        kb.act(PT, ps[:, 0:256], AF.Exp, [pb], [PTb], scale=0.125)
        state[i] = (PT, PTb)

    def stage2(i):
        h, p, d, r, u = tiles[i]
        acc, accb, qt, qb, kt, kbf = hd[h]
        vt, vb = hv[(h, p)]
        NU = (T // d) // 128
        PT, PTb = state.pop(i)
        po, pob = prO.next()
        kb.mm(po[:, 0:128], vt[:, r * (1 + NU) + u, :], PT[:, 0:128], True, False, [vb, PTb], [pob])
        kb.mm(po[:, 0:128], vt[:, r * (1 + NU) + u + 1, :], PT[:, 128:256], False, True, [vb, PTb], [pob])
        q0 = r + 128 * u * d
        av = acc[:, q0:q0 + 127 * d + 1:d]
        abs_ = accb[u * d:(u + 1) * d]
        if p == 0:
            kb.copy("act", av, po[:, 0:128], [pob], abs_)
        else:
            kb.tt("dve", av, po[:, 0:128], av, ALU.add, [pob] + abs_, abs_)
        if last_of_head[h] == i:
            for c0 in range(0, T, OC):
                rd, rdb, _ = rds.next()
                kb.recip(rd, acc[64:128, c0:c0 + OC], accb, [rdb])
                ot, ob, osem = outs.next()
                kb.tt("dve", ot, acc[0:64, c0:c0 + OC], rd, ALU.mult, accb + [rdb], [ob])
                kb.dma(io["mixT"][h * 64:(h + 1) * 64, c0:c0 + OC], ot, osem, reads=[ob], touch=[obuf])

    N = len(tiles)
    for i in range(N + LA):
        if i < N:
            stage1(i)
        if i >= LA:
            stage2(i - LA)
    return obuf


def t5_bucket_np(dist):
    max_exact = 16
    dd = np.maximum(dist, 1).astype(np.float32)
    large = max_exact + (np.log(dd / max_exact) / math.log(2048 / max_exact) * (32 - max_exact)).astype(np.int32)
    return np.where(dist < max_exact, dist, np.minimum(large, 31)).astype(np.int32)


def dil_onehot():
    oh = np.zeros((3, 32, 129), np.float32)
    for p, d in enumerate(A_PAT):
        b = t5_bucket_np(np.arange(129) * d)
        oh[p, b, np.arange(129)] = 1.0
    return oh


def phase_gla(kb, T, io, noctx=False, last_only=False):
    NCT = T // 128
    NT = 2 * NCT
    consts = make_consts(kb)
    cb_ = consts["buf"]
    csem = kb.new_sem("gconst")
    ones32 = kb.alloc([128, 128], F32)
    kb.memset("dve", ones32, 1.0 / 16.0, [cb_])
    triUs = kb.alloc([128, 128], F32)
    triLs = kb.alloc([128, 128], F32)
    triU1 = kb.alloc([128, 128], F32)
    kb.op("pool", lambda e: e.affine_select(out=triUs, in_=ones32, pattern=[[1, 128]], compare_op=ALU.is_ge, fill=0.0,
                                            base=0, channel_multiplier=-1), [cb_], [cb_])
    kb.tt("dve", triLs, ones32, triUs, ALU.subtract, [cb_], [cb_])
    kb.ts("dve", triU1, triUs, 16.0, None, ALU.mult, None, [cb_], [cb_])
    ctx01 = kb.alloc([128, 1], F32)
    kb.dma(ctx01, io["ctx01"], csem, writes=[cb_])
    w2s = kb.alloc([16, C_KW], F32)
    kb.dma(w2s, io["w2"], csem, writes=[cb_])
    w2b = kb.alloc([16, C_KW], BF16)
    kb.copy("dve", w2b, w2s, [cb_], [cb_])
    cgb = kb.alloc([128, C_KW], F32)
    kb.dma(cgb, io["cgb"].partition_broadcast(128), csem, writes=[cb_])
    cn = kb.alloc([128, C_DV], F32)
    kb.dma(cn, io["cn"].partition_broadcast(128), csem, writes=[cb_])
    gls = kb.alloc([16, 2 * T], F32)
    kb.dma(gls[:, 0:T], io["glT_ctx"], csem, writes=[cb_])
    kb.dma(gls[:, T:2 * T], io["glT"], csem, writes=[cb_])
    ident = kb.alloc([128, 128], BF16)
    onesb = kb.alloc([128, 128], BF16)
    kb.memset("dve", onesb, 1.0, [cb_])
    kb.op("pool", lambda e: e.affine_select(out=ident, in_=onesb, pattern=[[-1, 128]], compare_op=ALU.is_equal, fill=0.0,
                                            base=0, channel_multiplier=1), [cb_], [cb_])
    ocTs = Ring(kb, "gocT", 2, [128, 6, 128], BF16)
    glb = kb.alloc([16, 2 * T], BF16)
    kb.copy("pool", glb, gls, [cb_], [cb_])
    S = kb.alloc([96, C_H, C_DV], F32); Sb = Buf("S")
    Sbf = kb.alloc([96, C_H, C_DV], BF16); Sbfb = Buf("Sbf")
    kb.memset("dve", S.rearrange("k h v -> k (h v)"), 0.0, [Sb])
    kb.memset("pool", Sbf.rearrange("k h v -> k (h v)"), 0.0, [Sbfb])
    kin = Ring(kb, "gk", 2, [128, C_H, C_DK], BF16)
    vin = Ring(kb, "gv", 2, [128, C_H, C_DV], BF16)
    qTin = Ring(kb, "gqT", 2, [96, C_H, 128], BF16)
    kTin = Ring(kb, "gkT", 2, [96, C_H, 128], BF16)
    rin = Ring(kb, "gr", 2, [128, C_VW], BF16)
    Gs = Ring(kb, "gG", 2, [128, C_KW], F32, dma=False)
    E3s = Ring(kb, "gE3", 2, [128, C_KW], F32, dma=False)
    khs = Ring(kb, "gkh", 2, [128, C_H, C_DK], BF16, dma=False)
    decs = Ring(kb, "gdec", 2, [96, C_H], F32, dma=False)
    E1s = Ring(kb, "gE1", 2, [96, 128], F32, dma=False)
    qts = Ring(kb, "gqt", 2, [96, 128], BF16, dma=False)
    kts = Ring(kb, "gkt", 2, [96, 128], BF16, dma=False)
    Ams = Ring(kb, "gAm", 2, [128, 128], BF16, dma=False)
    sqs = Ring(kb, "gsq", 2, [128, C_DV], F32, dma=False)
    sss = Ring(kb, "gss", 4, [128, 1], F32, dma=False)
    sils = Ring(kb, "gsil", 2, [128, C_VW], F32, dma=False)
    ocs = Ring(kb, "goc", 2, [128, C_VW], BF16)
    pr = PsRing(kb, [0, 1, 2, 3, 4, 5, 6, 7])
    obuf = Buf("glaout")
    for n in range(NCT if noctx else 0, NT):
        own = n >= NCT
        emit = own and not (last_only and n != NT - 1)
        ps, pb = pr.next()
        kb.mm(ps[:, 0:C_KW], glb[:, n * 128:(n + 1) * 128], w2b, True, True, [cb_], [pb])
        G, Gb, _ = Gs.next()
        kb.tt("dve", G, ps[:, 0:C_KW], cgb, ALU.add, [pb, cb_], [Gb])
        kb.act(G, G, AF.Exp, [Gb], [Gb], scale=-1.0)
        kb.act(G, G, AF.Ln, [Gb], [Gb], bias=1.0)
        kt, ktb, ksem = kin.next()
        ksrc = io["kC_ctx"][n * 128:(n + 1) * 128] if n < NCT else io["kC"][(n - NCT) * 128:(n - NCT + 1) * 128]
        kb.dma(kt, ksrc.rearrange("p (h k) -> p h k", h=C_H), ksem, writes=[ktb])
        vt, vtb, vsem = vin.next()
        vsrc = io["vC_ctx"][n * 128:(n + 1) * 128] if n < NCT else io["vC"][(n - NCT) * 128:(n - NCT + 1) * 128]
        kb.dma(vt, vsrc.rearrange("p (h v) -> p h v", h=C_H), vsem, writes=[vtb])
        if not own:
            vf = vt.rearrange("p h v -> p (h v)")
            kb.ts("pool", vf, vf, ctx01[:, 0:1], None, ALU.mult, None, [vtb, cb_], [vtb])
        ps, pb = pr.next()
        kb.mm(ps[:, 0:C_KW], triLs, G, True, True, [Gb, cb_], [pb])
        E3, E3b, _ = E3s.next()
        kb.act(E3, ps[:, 0:C_KW], AF.Exp, [pb], [E3b], scale=-1.0)
        kh, khb, _ = khs.next()
        kb.tt("dve", kh.rearrange("p h k -> p (h k)"), kt.rearrange("p h k -> p (h k)"), E3, ALU.mult, [ktb, E3b], [khb])
        psd, pdb = pr.next()
        for h in range(C_H):
            kb.mm(psd[0:96, h:h + 1], G[:, h * 96:(h + 1) * 96], ones32[:, 0:1], True, True, [Gb, cb_], [pdb])
        dec, decb, _ = decs.next()
        kb.act(dec, psd[0:96, 0:C_H], AF.Exp, [pdb], [decb], scale=-1.0)
        if emit:
            c = n - NCT
            qT, qTb, qsem = qTin.next()
            kb.dma(qT, io["qCT"].rearrange("(h k) t -> k h t", h=C_H)[:, :, c * 128:(c + 1) * 128], qsem, writes=[qTb])
            kT, kTb, kTsem = kTin.next()
            kb.dma(kT, io["kCT"].rearrange("(h k) t -> k h t", h=C_H)[:, :, c * 128:(c + 1) * 128], kTsem, writes=[kTb])
            rt, rtb, rsem = rin.next()
            kb.dma(rt, io["rC"][c * 128:(c + 1) * 128, :], rsem, writes=[rtb])
            sil, silb, _ = sils.next()
            kb.act(sil, rt, AF.Silu, [rtb], [silb])
            oc, ocb, ocsem = ocs.next()
            for h in range(C_H):
                psb, pbb = pr.next()
                kb.mm(psb[0:96, 0:128], G[:, h * 96:(h + 1) * 96], triUs, True, True, [Gb, cb_], [pbb])
                E1, E1b, _ = E1s.next()
                kb.act(E1, psb[0:96, 0:128], AF.Exp, [pbb], [E1b], scale=-1.0)
                qt_, qtb_, _ = qts.next()
                kb.stt("dve", qt_, qT[:, h, :], float(C_DK) ** -0.5, E1, ALU.mult, ALU.mult, [qTb, E1b], [qtb_])
                E2, E2b, _ = E1s.next()
                kb.act(E2, psb[0:96, 0:128], AF.Exp, [pbb], [E2b])
                kt_, ktb_, _ = kts.next()
                kb.tt("dve", kt_, kT[:, h, :], E2, ALU.mult, [kTb, E2b], [ktb_])
                psa, pab = pr.next()
                kb.mm(psa[:, 0:128], kt_, qt_, True, True, [ktb_, qtb_], [pab])
                Am, Amb, _ = Ams.next()
                kb.tt("dve", Am, psa[:, 0:128], triU1, ALU.mult, [pab, cb_], [Amb])
                pso, pob = pr.next()
                kb.mm(pso[:, 0:C_DV], qt_, Sbf[:, h, :], True, False, [qtb_, Sbfb], [pob])
                kb.mm(pso[:, 0:C_DV], Am, vt[:, h, :], False, True, [Amb, vtb], [pob])
                sq, sqb, _ = sqs.next()
                kb.act(sq, pso[:, 0:C_DV], AF.Square, [pob], [sqb])
                ss, ssb, _ = sss.next()
                kb.op("dve", lambda e, ss=ss, sq=sq: e.reduce_sum(out=ss, in_=sq, axis=AX.X), [sqb], [ssb])
                kb.act(ss, ss, AF.Sqrt, [ssb, cb_], [ssb], scale=1.0 / C_DV, bias=consts["eps"])
                kb.recip(ss, ss, [ssb], [ssb])
                kb.stt("dve", sq, pso[:, 0:C_DV], ss[:, 0:1], cn, ALU.mult, ALU.mult, [pob, ssb, cb_], [sqb])
                kb.tt("dve", oc[:, h * C_DV:(h + 1) * C_DV], sq, sil[:, h * C_DV:(h + 1) * C_DV], ALU.mult, [sqb, silb], [ocb])
            ocT, ocTb, ocTsem = ocTs.next()
            for j in range(6):
                pst, ptb = pr.next()
                pv = pst.bitcast(BF16)[:, 0:128]
                kb.tr(pv, oc[:, j * 128:(j + 1) * 128], ident, [ocb, cb_], [ptb])
                kb.copy(("dve", "act")[j & 1], ocT[:, j, :], pv, [ptb], [ocTb])
            kb.dma(io["mixT"][A_W + B_W:D, c * 128:(c + 1) * 128].rearrange("(j p) t -> p j t", p=128), ocT, ocTsem,
                   reads=[ocTb], touch=[obuf])
        for h in range(C_H):
            pss, psb_ = pr.next()
            kb.mm(pss[0:96, 0:C_DV], kh[:, h, :], vt[:, h, :], True, True, [khb, vtb], [psb_])
            kb.stt("dve", S[:, h, :], S[:, h, :], dec[:, h:h + 1], pss[0:96, 0:C_DV], ALU.mult, ALU.add, [Sb, decb, psb_], [Sb])
        kb.copy("pool", Sbf.rearrange("k h v -> k (h v)"), S.rearrange("k h v -> k (h v)"), [Sb], [Sbfb])
    return obuf


P1_OUT = [(n, "fm", r * nb) for n, c0, r, nb in FM_GROUPS] + [(n, "tm", tot) for n, c0, tot, cw in TM_GROUPS]


def build_fused(T, depth=2):
    nc = bass.Bass("TRN2", target_bir_lowering=False)
    X = {}

    def inp(name, shape, dt=F32):
        X[name] = nc.dram_tensor(name, list(shape), dt, kind="ExternalInput").ap()

    def scr(name, shape, dt):
        return nc.dram_tensor(name, list(shape), dt).ap()

    inp("xTA", [D, T]); inp("xTB", [D, T]); inp("memT", [D, MEM])
    inp("ctx01A", [128, 1]); inp("ctx01B", [128, 1]); inp("ctxnegA", [128, 1]); inp("ctxnegB", [128, 1])
    inp("maskB", [4, 128, 512], BF16); inp("OH", [3, 32, 129]); inp("rel", [32, A_H])
    inp("norm_mix", [depth, D]); inp("w_in", [depth, D, PROJ_W]); inp("f_bias", [depth, B_H])
    inp("c_gate_w2", [depth, C_RANK, C_KW]); inp("c_gate_b", [depth, C_KW]); inp("c_norm", [depth, C_DV])
    inp("w_out", [depth, D, D]); inp("norm_cross", [depth, D]); inp("w_cq", [depth, D, X_W]); inp("w_ckv", [depth, D, 2 * X_W])
    inp("w_co", [depth, X_W, D]); inp("norm_ffn", [depth, D]); inp("w_up", [depth, D, 2 * DFF]); inp("conv_w", [depth, 3, DFF])
    inp("conv_b", [depth, DFF]); inp("w_down", [depth, DFF, D]); inp("mem_norm", [D]); inp("norm_final", [D])
    yT = nc.dram_tensor("yT", [D, T], F32, kind="ExternalOutput").ap()
    P = {}
    for ps_ in "AB":
        P[ps_] = {}
        for name, kind, n in P1_OUT:
            dt = F32 if name in ("glT", "fl") else BF16
            P[ps_][name] = scr("%s_%s" % (name, ps_), [n, T] if kind == "fm" else [T, n], dt)
    mixT = scr("mixT", [D, T], BF16)
    x2 = {"A": scr("x2A", [D, T], F32), "B": scr("x2B", [D, T], F32)}
    xn = {"A": scr("xnA", [D, T], F32), "B": scr("xnB", [D, T], F32)}
    uid = [0]
    WN = (("w_in", [D, PROJ_W]), ("w_out", [D, D]), ("w_cq", [D, X_W]), ("w_ckv", [D, 2 * X_W]), ("w_co", [X_W, D]),
          ("w_up", [D, 2 * DFF]), ("w_down", [DFF, D]))
    W = [{n: scr("%s_bf%d" % (n, l), shp, BF16) for n, shp in WN} for l in range(depth)]
    with contextlib.ExitStack() as st:
        kb = KB(nc, st)
        xcur = {"A": X["xTA"], "B": X["xTB"]}
        for l in range(depth):
            kb.phase()
            phase_cast(kb, [(X[n][l], W[l][n]) for n, shp in WN])
            for ps_ in "AB":
                if ps_ == "A" and False:
                    continue
                own, ctx = P[ps_], P["A"]
                isA = ps_ == "A"
                lastA = isA and l == depth - 1
                c01, cneg = X["ctx01" + ps_], X["ctxneg" + ps_]
                kb.phase()
                io = {"xT": xcur[ps_], "g1": X["norm_mix"][l], "w_in": W[l]["w_in"]}
                io.update(own)
                phase_p1(kb, T, io)
                kb.phase()
                phase_fox(kb, T, {"qBT": own["qBT"], "kBT": own["kBT"], "kBT_ctx": ctx["kBT"], "vB": own["vB"], "vB_ctx": ctx["vB"],
                                  "fl": own["fl"], "fl_ctx": ctx["fl"], "fb": X["f_bias"][l], "ctxneg": cneg, "maskB": X["maskB"],
                                  "mixT": mixT}, noctx=isA, last_only=lastA)
                kb.phase()
                uid[0] += 1
                phase_dil(kb, nc, T, {"qAT": own["qAT"], "kAT": own["kAT"], "kAT_ctx": ctx["kAT"], "vA": own["vA"], "vA_ctx": ctx["vA"],
                                      "OH": X["OH"], "rel": X["rel"], "ctx01": c01, "ctxneg": cneg, "mixT": mixT}, uid[0], last_only=lastA)
                kb.phase()
                phase_gla(kb, T, {"glT": own["glT"], "glT_ctx": ctx["glT"], "w2": X["c_gate_w2"][l], "cgb": X["c_gate_b"][l],
                                  "qCT": own["qCT"], "kCT": own["kCT"], "kC": own["kC"], "kC_ctx": ctx["kC"], "vC": own["vC"],
                                  "vC_ctx": ctx["vC"], "rC": own["rC"], "cn": X["c_norm"][l], "ctx01": c01, "mixT": mixT}, noctx=isA, last_only=lastA)
                kb.phase()
                phase_p2b(kb, T, {"mixT": mixT, "xT": xcur[ps_], "w_out": W[l]["w_out"], "g2": X["norm_cross"][l], "w_cq": W[l]["w_cq"],
                                  "w_ckv": W[l]["w_ckv"], "w_co": W[l]["w_co"], "memT": X["memT"], "gM": X["mem_norm"], "x2T": x2[ps_]},
                          last_only=lastA)
                final = l == depth - 1
                if ps_ == "A" and final:
                    continue
                kb.phase()
                phase_p3(kb, T, {"x2T": x2[ps_], "halo": x2["A"][:, T - 2:T], "ctx01": c01, "g3": X["norm_ffn"][l], "gF": X["norm_final"],
                                 "w_up": W[l]["w_up"], "conv_w": X["conv_w"][l], "conv_b": X["conv_b"][l], "w_down": W[l]["w_down"],
                                 "yT": yT if (final and ps_ == "B") else xn[ps_]}, final and ps_ == "B")
            xcur = {"A": xn["A"], "B": xn["B"]}
        kb.finalize()
        print("fused program: %d instructions, %d semaphores" % (kb.n, len(kb.sems)), flush=True)
    return nc


def kernel(x, mem, rel_table, mem_norm, norm_final, norm_mix, w_in, f_bias, c_gate_w2, c_gate_b, c_norm, w_out,
           norm_cross, w_cq, w_ckv, w_co, norm_ffn, w_up, conv_w, conv_b, w_down):
    import ml_dtypes
    f = lambda a: np.ascontiguousarray(np.asarray(a, dtype=np.float32))
    x, mem = f(x), f(mem)
    Bn, S, _ = x.shape
    T = S // 2
    depth = np.asarray(w_in).shape[0]
    cores = [(b, h) for b in range(Bn) for h in range(2)]
    maskB = np.zeros((4, 128, 512), np.float32)
    for kk in range(4):
        maskB[kk] = (128 * kk + np.arange(128)[:, None] <= np.arange(512)[None, :])
    maskB = maskB.astype(ml_dtypes.bfloat16)
    shared = {"maskB": maskB, "OH": dil_onehot(), "rel": f(rel_table), "norm_mix": f(norm_mix), "w_in": f(w_in), "f_bias": f(f_bias),
              "c_gate_w2": f(c_gate_w2), "c_gate_b": f(c_gate_b), "c_norm": f(c_norm), "w_out": f(w_out), "norm_cross": f(norm_cross),
              "w_cq": f(w_cq), "w_ckv": f(w_ckv), "w_co": f(w_co), "norm_ffn": f(norm_ffn), "w_up": f(w_up), "conv_w": f(conv_w),
              "conv_b": f(conv_b), "w_down": f(w_down), "mem_norm": f(mem_norm), "norm_final": f(norm_final),
              "ctx01A": np.zeros((128, 1), np.float32), "ctxnegA": np.full((128, 1), NEG, np.float32)}
    memT = [np.ascontiguousarray(mem[b].T) for b in range(Bn)]
    xTh = {(b, h): np.ascontiguousarray(x[b, h * T:(h + 1) * T].T) for b, h in cores}
    in_maps = []
    for b, h in cores:
        m = dict(shared)
        m.update({"xTA": xTh[(b, 0)], "xTB": xTh[(b, h)], "memT": memT[b],
                  "ctx01B": np.full((128, 1), float(h), np.float32),
                  "ctxnegB": np.full((128, 1), 0.0 if h == 1 else NEG, np.float32)})
        in_maps.append(m)
    nc = build_fused(T, depth)
    res = run_bass_kernel_spmd(nc, in_maps, core_ids=list(range(len(cores))))
    out = np.empty((Bn, S, D), np.float32)
    for c, (b, h) in enumerate(cores):
        out[b, h * T:(h + 1) * T] = np.asarray(res.results[c]["yT"]).T
    return out
```
